# Optimizing a Trainium2 kernel written in Bass

```python
import jax, jax.numpy as jnp
from jax import lax
import numpy as np

D_MODEL = 1024
BATCH = 8
SEQ = 4096
DEPTH = 1
DEC_BATCH = 8
DEC_SEQ = 8192
PAST_LEN = 128

D_POOL = D_MODEL // 2
D_RWKV = D_MODEL - D_POOL
POOL_WINDOWS = (2, 4, 8, 16)
N_POOL_GROUPS = len(POOL_WINDOWS)
POOL_GROUP = D_POOL // N_POOL_GROUPS
HEAD_DIM = 64
N_HEADS = D_RWKV // HEAD_DIM
R_DECAY = 64
R_AAA = 64
R_GATE = 160
N_DIR = 2
D_FF = 2816
P_RWKV = 3 * D_RWKV + N_DIR * R_DECAY + N_DIR * R_AAA + R_GATE
P_IN = D_POOL + P_RWKV
N_MOD = 9
RMS_EPS = 1e-6
GN_EPS = 64e-5
L2_EPS = 1e-12

kernel_name = "hybrid_pool_rwkv7_macaron_encoder"


def rmsnorm(x, g):
    x32 = x.astype(jnp.float32)
    y = x32 * lax.rsqrt(jnp.mean(x32 * x32, axis=-1, keepdims=True) + RMS_EPS) * g.astype(jnp.float32)
    return y.astype(x.dtype)


def modulate(h, shift, scale):
    return h * (1 + scale) + shift


def swiglu(h, w1, w3, w2):
    return (jax.nn.silu(h @ w1) * (h @ w3)) @ w2


def token_shift(z, mu):
    prev = jnp.pad(z[:, :-1], ((0, 0), (1, 0), (0, 0)))
    nxt = jnp.pad(z[:, 1:], ((0, 0), (0, 1), (0, 0)))
    return z + (0.5 * (prev + nxt) - z) * mu


def pool_mixer(z, pool_w, pool_scale):
    B, T, _ = z.shape
    z = z.astype(jnp.float32)
    cs = jnp.concatenate([jnp.zeros((B, 1, D_POOL), jnp.float32), jnp.cumsum(z, axis=1)], axis=1)
    t = jnp.arange(T)
    outs = []
    for gi, win in enumerate(POOL_WINDOWS):
        sl = slice(gi * POOL_GROUP, (gi + 1) * POOL_GROUP)
        lo = jnp.clip(t - win // 2, 0, T)
        hi = jnp.clip(t + win // 2, 0, T)
        csg = cs[..., sl]
        wsum = jnp.take(csg, hi, axis=1) - jnp.take(csg, lo, axis=1)
        count = (hi - lo).astype(jnp.float32)[None, :, None]
        outs.append(wsum / count - z[..., sl])
    p = jnp.stack(outs, axis=2)
    p = jnp.einsum('btgc,gcd->btgd', p, pool_w.astype(jnp.float32))
    return p.reshape(B, T, D_POOL) * pool_scale.astype(jnp.float32)


def wkv_scan(r, decay, k, v, kk, a, reverse):
    B, T, H, N = r.shape
    xs = tuple(jnp.moveaxis(t_, 1, 0) for t_ in (r, decay, k, v, -kk, kk * a))

    def step(S, inp):
        r_t, w_t, k_t, v_t, na_t, b_t = inp
        sa = jnp.einsum('bhvk,bhk->bhv', S, na_t)
        S = S * w_t[:, :, None, :] + sa[..., None] * b_t[:, :, None, :] + v_t[..., None] * k_t[:, :, None, :]
        y = jnp.einsum('bhvk,bhk->bhv', S, r_t)
        return S, y

    S0 = jnp.zeros((B, H, N, N), jnp.float32)
    _, y = lax.scan(step, S0, xs, reverse=reverse)
    return jnp.moveaxis(y, 0, 1)


def rwkv7_mixer(z, w0, w2, a0, a2, g2, k_k, k_a, r_k, lnx_g, lnx_b):
    B, T, _ = z.shape
    f32 = jnp.float32
    z = z.astype(f32)
    hd = lambda t_: t_.reshape(t_.shape[:-1] + (N_HEADS, HEAD_DIM))
    r = hd(z[..., 0:D_RWKV])
    k = hd(z[..., D_RWKV:2 * D_RWKV])
    v = hd(z[..., 2 * D_RWKV:3 * D_RWKV])
    o = 3 * D_RWKV
    zw = (z[..., o:o + R_DECAY], z[..., o + R_DECAY:o + 2 * R_DECAY])
    o += 2 * R_DECAY
    za = (z[..., o:o + R_AAA], z[..., o + R_AAA:o + 2 * R_AAA])
    o += 2 * R_AAA
    zg = z[..., o:o + R_GATE]
    kkh = hd(k_k.astype(f32))
    kah = hd(k_a.astype(f32))
    rkh = hd(r_k.astype(f32))
    kk = k * kkh
    kk = kk / jnp.maximum(jnp.sqrt(jnp.sum(kk * kk, axis=-1, keepdims=True)), L2_EPS)
    ys = []
    bonus = []
    for d in range(N_DIR):
        w_log = -jax.nn.softplus(-(w0[d].astype(f32) + jnp.tanh(zw[d]) @ w2[d].astype(f32))) - 0.5
        decay = hd(jnp.exp(-jnp.exp(w_log)))
        a = hd(jax.nn.sigmoid(a0[d].astype(f32) + za[d] @ a2[d].astype(f32)))
        k_d = k * (1 + (a - 1) * kah)
        ys.append(wkv_scan(r, decay, k_d, v, kk, a, reverse=(d == 1)))
        bonus.append(jnp.sum(r * k_d * rkh, axis=-1, keepdims=True) * v)
    y = ys[0] + ys[1]
    mean = jnp.mean(y, axis=-1, keepdims=True)
    var = jnp.mean(jnp.square(y - mean), axis=-1, keepdims=True)
    y = (y - mean) * lax.rsqrt(var + GN_EPS) * hd(lnx_g.astype(f32)) + hd(lnx_b.astype(f32))
    g = jax.nn.sigmoid(zg) @ g2.astype(f32)
    return (y + bonus[0] + bonus[1]).reshape(B, T, D_RWKV) * g


def encoder_layer(x, c, ada_w, ada_b, n1_pre, n1_post, f1_w1, f1_w3, f1_w2,
                  nm_pre, nm_post, w_in, shift_mu, pool_w, pool_scale,
                  w0, w2, a0, a2, g2, k_k, k_a, r_k, lnx_g, lnx_b, w_out,
                  n2_pre, n2_post, f2_w1, f2_w3, f2_w2):
    B = x.shape[0]
    mod = (jax.nn.silu(c) @ ada_w + ada_b).reshape(B, N_MOD, D_MODEL)[:, :, None, :]
    h = modulate(rmsnorm(x, n1_pre), mod[:, 0], mod[:, 1])
    x = x + 0.5 * mod[:, 2] * rmsnorm(swiglu(h, f1_w1, f1_w3, f1_w2), n1_post)
    h = modulate(rmsnorm(x, nm_pre), mod[:, 3], mod[:, 4])
    z = h @ w_in
    zp = z[..., :D_POOL]
    zr = token_shift(z[..., D_POOL:], shift_mu)
    m = jnp.concatenate([pool_mixer(zp, pool_w, pool_scale),
                         rwkv7_mixer(zr, w0, w2, a0, a2, g2, k_k, k_a, r_k, lnx_g, lnx_b)], axis=-1)
    m = m.astype(x.dtype) @ w_out
    x = x + mod[:, 5] * rmsnorm(m, nm_post)
    h = modulate(rmsnorm(x, n2_pre), mod[:, 6], mod[:, 7])
    x = x + 0.5 * mod[:, 8] * rmsnorm(swiglu(h, f2_w1, f2_w3, f2_w2), n2_post)
    return x


def encoder_trunk(x, c, params):
    for i in range(DEPTH):
        x = encoder_layer(x, c, *[p[i] for p in params])
    return x


def setup_inputs(seed: int = 0) -> dict:
    key = jax.random.key(seed)
    ks = iter(jax.random.split(key, 48))
    f32 = jnp.float32
    nrm = lambda shape, scale: scale * jax.random.normal(next(ks), shape, f32)
    L, D, F = DEPTH, D_MODEL, D_FF
    return {
        "x_prompt": nrm((BATCH, SEQ, D), 1.0),
        "x_sample": nrm((DEC_BATCH, DEC_SEQ, D), 1.0),
        "c_prompt": nrm((BATCH, D), 1.0),
        "c_sample": nrm((DEC_BATCH, D), 1.0),
        "ada_w": nrm((L, D, N_MOD * D), 0.5 * D ** -0.5),
        "ada_b": nrm((L, N_MOD * D), 0.01),
        "n1_pre": 1.0 + nrm((L, D), 0.05),
        "n1_post": 1.0 + nrm((L, D), 0.05),
        "f1_w1": nrm((L, D, F), D ** -0.5),
        "f1_w3": nrm((L, D, F), D ** -0.5),
        "f1_w2": nrm((L, F, D), F ** -0.5),
        "nm_pre": 1.0 + nrm((L, D), 0.05),
        "nm_post": 1.0 + nrm((L, D), 0.05),
        "w_in": nrm((L, D, P_IN), D ** -0.5),
        "shift_mu": jax.random.uniform(next(ks), (L, P_RWKV), f32),
        "pool_w": nrm((L, N_POOL_GROUPS, POOL_GROUP, POOL_GROUP), POOL_GROUP ** -0.5),
        "pool_scale": 1.0 + nrm((L, D_POOL), 0.1),
        "w0": -1.0 + nrm((L, N_DIR, D_RWKV), 0.5),
        "w2": nrm((L, N_DIR, R_DECAY, D_RWKV), R_DECAY ** -0.5),
        "a0": nrm((L, N_DIR, D_RWKV), 0.5),
        "a2": nrm((L, N_DIR, R_AAA, D_RWKV), 0.5 * R_AAA ** -0.5),
        "g2": nrm((L, R_GATE, D_RWKV), R_GATE ** -0.5),
        "k_k": 0.85 + nrm((L, D_RWKV), 0.05),
        "k_a": 1.0 + nrm((L, D_RWKV), 0.05),
        "r_k": nrm((L, D_RWKV), 0.5),
        "lnx_g": 1.0 + nrm((L, D_RWKV), 0.05),
        "lnx_b": nrm((L, D_RWKV), 0.01),
        "w_out": nrm((L, D, D), D ** -0.5),
        "n2_pre": 1.0 + nrm((L, D), 0.05),
        "n2_post": 1.0 + nrm((L, D), 0.05),
        "f2_w1": nrm((L, D, F), D ** -0.5),
        "f2_w3": nrm((L, D, F), D ** -0.5),
        "f2_w2": nrm((L, F, D), F ** -0.5),
    }


def reference(x_prompt, x_sample, c_prompt, c_sample, ada_w, ada_b, n1_pre, n1_post, f1_w1, f1_w3, f1_w2,
              nm_pre, nm_post, w_in, shift_mu, pool_w, pool_scale,
              w0, w2, a0, a2, g2, k_k, k_a, r_k, lnx_g, lnx_b, w_out,
              n2_pre, n2_post, f2_w1, f2_w3, f2_w2):
    params = (ada_w, ada_b, n1_pre, n1_post, f1_w1, f1_w3, f1_w2,
              nm_pre, nm_post, w_in, shift_mu, pool_w, pool_scale,
              w0, w2, a0, a2, g2, k_k, k_a, r_k, lnx_g, lnx_b, w_out,
              n2_pre, n2_post, f2_w1, f2_w3, f2_w2)
    y_prompt = encoder_trunk(x_prompt, c_prompt, params)
    y_sample = encoder_trunk(x_sample, c_sample, params)
    return (y_prompt, y_sample)
```

```python
from contextlib import ExitStack
import numpy as np
import ml_dtypes
import concourse.bass as bass
import concourse.mybir as mybir
from concourse.bass_utils import run_bass_kernel_spmd

F32 = mybir.dt.float32
BF16 = mybir.dt.bfloat16
ALU = mybir.AluOpType
AF = mybir.ActivationFunctionType

D = 1024
FF = 2816
NJ = 22
NZ = 20
TT = 512
CH = 128
CDEC = float(np.exp(-0.5))
SELF_SYNC = True


class Eng:
    def __init__(self, name, eng, sem):
        self.name, self.e, self.sem, self.cnt, self.seen = name, eng, sem, 0, {}


class Tl:
    def __init__(self, name, const=False):
        self.name, self.w, self.r, self.const = name, None, {}, const
        self.dsem, self.dcnt = None, 0
        self.psum = False


class KB:
    def __init__(self, nc, es):
        self.nc, self.es = nc, es
        mk = lambda n, e: Eng(n, e, es.enter_context(nc.semaphore("sem_" + n)))
        self.PE, self.ACT, self.DVE = mk("pe", nc.tensor), mk("act", nc.scalar), mk("dve", nc.vector)
        self.POOL, self.SP = mk("pool", nc.gpsimd), mk("sp", nc.sync)
        self.nsem = 0
        self.dma_tiles = []

    def tl(self, name, const=False, dma=False):
        t = Tl(name, const)
        if dma:
            t.name = "%s#%d" % (name, self.nsem)
            t.dsem = self.es.enter_context(self.nc.semaphore("ds_%d" % self.nsem))
            self.nsem += 1
            self.dma_tiles.append(t)
        return t

    def tls(self, name, n, **kw):
        return [self.tl("%s%d" % (name, i), **kw) for i in range(n)]

    def _wait(self, E, tk):
        if tk is None:
            return
        key, sem, val = tk
        if key == E.name and (E is self.PE or not SELF_SYNC):
            return
        if E.seen.get(key, 0) >= val:
            return
        E.e.wait_ge(sem, val)
        E.seen[key] = val

    def deps(self, E, reads, writes):
        for t in reads:
            self._wait(E, t.w)
            if t.psum:
                for key, (sem, val) in t.r.items():
                    if key != E.name:
                        self._wait(E, (key, sem, val))
        for t in writes:
            self._wait(E, t.w)
            for key, (sem, val) in t.r.items():
                self._wait(E, (key, sem, val))

    def _commit(self, tk, reads, writes):
        key, sem, val = tk
        for t in reads:
            if not t.const:
                t.r[key] = (sem, val)
        for t in writes:
            t.w = tk
            t.r = {}

    def op(self, E, reads, writes, fn):
        self.deps(E, reads, writes)
        ins = fn()
        E.cnt += 1
        ins.then_inc(E.sem, 1)
        self._commit((E.name, E.sem, E.cnt), reads, writes)

    def grp(self, E, reads, writes, fns):
        self.deps(E, reads, writes)
        ins = None
        for f in fns:
            ins = f()
        E.cnt += 1
        ins.then_inc(E.sem, 1)
        self._commit((E.name, E.sem, E.cnt), reads, writes)

    def dma(self, out, in_, reads, writes, dt, **kw):
        E = self.SP
        self.deps(E, reads, writes)
        ins = E.e.dma_start(out=out, in_=in_, **kw)
        dt.dcnt += 16
        ins.then_inc(dt.dsem, 16)
        self._commit((("d", dt.name), dt.dsem, dt.dcnt), reads, writes)

    def barrier(self):
        SP = self.SP
        for t in self.dma_tiles:
            if t.dcnt:
                self._wait(SP, (("d", t.name), t.dsem, t.dcnt))
        SP.e.sem_inc(SP.sem, 1)
        SP.cnt += 1
        engs = [self.PE, self.ACT, self.DVE, self.POOL, SP]
        snap = {E.name: E.cnt for E in engs}
        for E in engs:
            for E2 in engs:
                if E2 is E and E is SP:
                    continue
                v = snap[E2.name]
                if v and E.seen.get(E2.name, 0) < v:
                    E.e.wait_ge(E2.sem, v)
                    E.seen[E2.name] = v

    def final_wait(self, tiles):
        for t in tiles:
            self._wait(self.SP, t.w)


DEBUG_OUT = set()


def build(TA, TB, stage="full"):
    nc = bass.Bass("TRN2", target_bir_lowering=False)
    es = ExitStack()
    with es:
        _build(nc, es, TA, TB, stage)
    return nc


def _build(nc, es, TA, TB, stage):
    k = KB(nc, es)
    PE, ACT, DVE, POOL = k.PE, k.ACT, k.DVE, k.POOL
    pe, act, dve, pool = nc.tensor, nc.scalar, nc.vector, nc.gpsimd
    seqT = [TA, TB]

    def din(name, shape, dt=F32):
        return nc.dram_tensor(name, list(shape), dt, kind="ExternalInput").ap()

    def dscr(name, shape, dt=F32):
        if name in DEBUG_OUT:
            return nc.dram_tensor(name, list(shape), dt, kind="ExternalOutput").ap()
        return nc.dram_tensor(name, list(shape), dt).ap()

    x_in = [din("xa", [TA, D]), din("xb", [TB, D])]
    y_out = [nc.dram_tensor("ya", [TA, D], F32, kind="ExternalOutput").ap(),
             nc.dram_tensor("yb", [TB, D], F32, kind="ExternalOutput").ap()]
    cpack_d = din("cpack", [128, 128])
    pack1_d = din("pack1", [128, 128])
    pack2_d = din("pack2", [128, 128])
    ada_w_d = din("ada_w", [D, 9 * D])
    fw1_d = [din("f1_w1", [D, FF]), din("f2_w1", [D, FF])]
    fw3_d = [din("f1_w3", [D, FF]), din("f2_w3", [D, FF])]
    fw2_d = [din("f1_w2", [FF, D]), din("f2_w2", [FF, D])]
    w_in_d = din("w_in", [D, 2464])
    w_out_d = din("w_out", [D, D])
    pool_w_d = din("pool_w", [4, 128, 128])
    w2r_d = din("w2r", [128, 512])
    a2r_d = din("a2r", [128, 512])
    g2p_d = din("g2p", [256, 512])
    ident_d = din("ident", [128, 128])
    cmask_d = din("cmask", [2, 128, 640])
    bones_d = din("bones", [128, 128])
    pcorr_d = din("pcorr", [128, 2, 4, 8])

    w13s = [dscr("w13s%d" % f, [NJ, 128, 2048], BF16) for f in range(2)]
    w2s = [dscr("w2s%d" % f, [8, 128, FF], BF16) for f in range(2)]
    wins = dscr("wins", [NZ, 128, 1024], BF16)
    wouts = dscr("wouts", [8, 128, 1024], BF16)
    if "x1T0" in DEBUG_OUT:
        x1T = [dscr("x1T%d" % s, [D, seqT[s]]) for s in range(2)]
    else:
        x1T = [y_out[s].rearrange("t d -> (t d)").rearrange("(f t) -> f t", t=seqT[s]) for s in range(2)]
    zT = [dscr("zT%d" % s, [NZ * 128, seqT[s]], BF16) for s in range(2)]
    ybT = [dscr("ybT%d" % s, [512, seqT[s]]) for s in range(2)]
    bbT = [dscr("bbT%d" % s, [512, seqT[s]]) for s in range(2)]
    if "x2T0" in DEBUG_OUT:
        x2T = [dscr("x2T%d" % s, [D, seqT[s]]) for s in range(2)]
    else:
        x2T = None

    def x2_views(s, c0, c1):
        if x2T is not None:
            v = x2T[s].rearrange("(kc p) t -> p kc t", p=128)
            return [(v[:, 0:4, c0:c1], 0), (v[:, 4:8, c0:c1], 4)]
        return [(ybT[s].rearrange("(kc p) t -> p kc t", p=128)[:, :, c0:c1], 0), (bbT[s].rearrange("(kc p) t -> p kc t", p=128)[:, :, c0:c1], 4)]
    dr = {}

    def drt(name):
        if name not in dr:
            dr[name] = k.tl("dr_" + name)
        return dr[name]

    PFX = ["s_"]

    def sb(name, shape, dt, stack=es):
        return stack.enter_context(nc.sbuf_tensor(PFX[0] + name, list(shape), dt))

    ps = [es.enter_context(nc.psum_tensor("ps%d" % i, [128, 512], F32)) for i in range(7)]
    psb = es.enter_context(nc.psum_tensor("psb", [128, 1024], BF16))
    ps_t = k.tls("ps", 7)
    for t_ in ps_t:
        t_.psum = True
    psb_t = k.tl("psb")

    ident = sb("ident", [128, 128], F32)
    ident_t = k.tl("ident", const=True, dma=True)
    identb = sb("identb", [128, 4, 128], BF16)
    identb_t = k.tl("identb", const=True)
    onesm = sb("onesm", [128, 128], BF16)
    onesm_t = k.tl("onesm", const=True)
    bones = sb("bones", [128, 128], F32)
    bones_t = k.tl("bones", const=True, dma=True)
    bonesb = sb("bonesb", [128, 128], BF16)
    bavgb = sb("bavgb", [128, 128], BF16)
    bonesb_t = k.tl("bonesb", const=True)
    pk1T = sb("pk1T", [128, 128], F32)
    pk2T = sb("pk2T", [128, 128], F32)
    pk_t = k.tl("pk", const=True)
    sc = sb("sc", [128, 2, 9, 8], F32)
    sc_t = k.tl("sc", const=True)
    ommu = sb("ommu", [128, 16], F32)
    hmu = sb("hmu", [128, 16], F32)
    epsb = sb("epsb", [128, 4], F32)
    cns_t = k.tl("cns", const=True)

    k.dma(ident[:, :], ident_d[:, :], [], [ident_t], ident_t)
    k.dma(bones[:, :], bones_d[:, :], [], [bones_t], bones_t)
    k.op(DVE, [], [cns_t], lambda: dve.memset(epsb[:, 0:1], 1e-6))
    k.op(DVE, [], [cns_t], lambda: dve.memset(epsb[:, 1:2], 64e-5))
    k.op(DVE, [], [cns_t], lambda: dve.memset(epsb[:, 2:3], 1e-18))
    k.op(DVE, [], [cns_t], lambda: dve.memset(epsb[:, 3:4], 0.0))
    k.op(DVE, [], [onesm_t], lambda: dve.memset(onesm[:, :], 1.0 / D))
    for q in range(4):
        k.op(DVE, [ident_t], [identb_t], lambda q=q: dve.tensor_copy(out=identb[:, q, :], in_=ident[:, :]))
    k.op(DVE, [bones_t], [bonesb_t], lambda: dve.tensor_copy(out=bonesb[:, :], in_=bones[:, :]))
    k.op(DVE, [bones_t], [bonesb_t], lambda: dve.tensor_scalar(out=bavgb[:, :], in0=bones[:, :], scalar1=1.0 / 64, scalar2=None, op0=ALU.mult))

    dbg_done = set()

    def dbg(name, ap, shape, dt, tls_):
        if name not in DEBUG_OUT or name in dbg_done:
            return
        dbg_done.add(name)
        dd = nc.dram_tensor(name, list(shape), dt, kind="ExternalOutput").ap()
        dt_ = k.tl(name, dma=True)
        k.dma(dd, ap, tls_, [dt_], dt_)

    def mm(out, lhsT, rhs, start, stop):
        return lambda: pe.matmul(out, lhsT, rhs, start=start, stop=stop)

    with ExitStack() as p0:
        stg = sb("p0stg", [128, 3, 128], F32, p0)
        stg_t = k.tl("p0stg", dma=True)
        k.dma(stg[:, 0, :], cpack_d[:, :], [], [stg_t], stg_t)
        k.dma(stg[:, 1, :], pack1_d[:, :], [], [stg_t], stg_t)
        k.dma(stg[:, 2, :], pack2_d[:, :], [], [stg_t], stg_t)
        cT = sb("cT", [128, 128], F32, p0)
        scT = sb("scT", [128, 16], F32, p0)
        cT_t = k.tl("cT")
        k.grp(PE, [stg_t, ident_t], [ps_t[0]], [lambda q=q: pe.transpose(ps[0][:, q * 128:(q + 1) * 128], stg[:, q, :], ident[:, :]) for q in range(3)])
        k.op(ACT, [ps_t[0]], [cT_t], lambda: act.activation(out=scT[:, :], in_=ps[0][:, 0:16], func=AF.Silu))
        k.op(DVE, [ps_t[0]], [pk_t], lambda: dve.tensor_copy(out=pk1T[:, :], in_=ps[0][:, 128:256]))
        k.op(DVE, [ps_t[0]], [pk_t], lambda: dve.tensor_copy(out=pk2T[:, :], in_=ps[0][:, 256:384]))
        aw = [sb("aw%d" % i, [128, 8, 1024], F32, p0) for i in range(2)]
        aw_t = k.tls("aw", 2, dma=True)
        awv = ada_w_d.rearrange("(kc p) n -> p kc n", p=128)
        scv = scT[:, :].rearrange("p (s kc) -> p s kc", s=2)
        for m in range(9):
            b = m % 2
            k.dma(aw[b][:, :, :], awv[:, :, m * 1024:(m + 1) * 1024], [], [aw_t[b]], aw_t[b])
            for oc in range(8):
                col = (m * 8 + oc) * 2
                k.grp(PE, [aw_t[b], cT_t], [ps_t[1]],
                      [mm(ps[1][:, col:col + 2], aw[b][:, kc, oc * 128:(oc + 1) * 128], scv[:, :, kc], kc == 0, kc == 7) for kc in range(8)])
        mod = sb("mod", [128, 2, 72], F32, p0)
        mod_t = k.tl("mod")
        psv = ps[1][:, 0:144].rearrange("p (j s) -> p s j", s=2)
        for s in range(2):
            k.op(DVE, [ps_t[1], pk_t], [mod_t], lambda s=s: dve.tensor_tensor(out=mod[:, s, :], in0=psv[:, s, :], in1=pk1T[:, 0:72], op=ALU.add))
        tmp8 = sb("tmp8", [128, 8], F32, p0)
        t8_t = k.tl("tmp8")
        for s in range(2):
            for gi, (pre, post, half) in enumerate([(72, 80, 0.5), (88, 96, 1.0), (104, 112, 0.5)]):
                msh, msc, mg = 3 * gi, 3 * gi + 1, 3 * gi + 2
                k.op(DVE, [mod_t], [t8_t], lambda s=s, msc=msc: dve.tensor_scalar(out=tmp8[:, :], in0=mod[:, s, msc * 8:msc * 8 + 8], scalar1=1.0, scalar2=None, op0=ALU.add))
                k.op(DVE, [t8_t, pk_t], [sc_t], lambda s=s, gi=gi, pre=pre: dve.tensor_tensor(out=sc[:, s, 3 * gi, :], in0=tmp8[:, :], in1=pk1T[:, pre:pre + 8], op=ALU.mult))
                k.op(DVE, [mod_t], [sc_t], lambda s=s, gi=gi, msh=msh: dve.tensor_copy(out=sc[:, s, 3 * gi + 1, :], in_=mod[:, s, msh * 8:msh * 8 + 8]))
                k.op(DVE, [mod_t, pk_t], [sc_t], lambda s=s, gi=gi, mg=mg, post=post, half=half: dve.scalar_tensor_tensor(
                    out=sc[:, s, 3 * gi + 2, :], in0=mod[:, s, mg * 8:mg * 8 + 8], scalar=half, in1=pk1T[:, post:post + 8], op0=ALU.mult, op1=ALU.mult))
        k.op(DVE, [pk_t], [cns_t], lambda: dve.tensor_scalar(out=ommu[:, :], in0=pk2T[:, 40:56], scalar1=-1.0, scalar2=1.0, op0=ALU.mult, op1=ALU.add))
        k.op(DVE, [pk_t], [cns_t], lambda: dve.tensor_scalar(out=hmu[:, :], in0=pk2T[:, 40:56], scalar1=0.5, scalar2=None, op0=ALU.mult))
        k.barrier()
    if "dbg_sc" in DEBUG_OUT:
        dsc = nc.dram_tensor("dbg_sc", [128, 144], F32, kind="ExternalOutput").ap()
        dsc_t = k.tl("dbgsc", dma=True)
        k.dma(dsc[:, :], sc[:, :, :, :].rearrange("p a b c -> p (a b c)"), [sc_t], [dsc_t], dsc_t)
        dpk = nc.dram_tensor("dbg_pk", [128, 256], F32, kind="ExternalOutput").ap()
        k.dma(dpk[:, 0:128], pk1T[:, :], [pk_t], [dsc_t], dsc_t)
        k.dma(dpk[:, 128:256], pk2T[:, :], [pk_t], [dsc_t], dsc_t)
    W0 = lambda j: pk2T[:, j:j + 1]
    A0 = lambda j: pk2T[:, 8 + j:9 + j]
    KK, KA, RK, LG, LB, PSC = 16, 20, 24, 28, 32, 36

    with ExitStack() as pw:
        NWB = 4
        wst = [sb("wst%d" % i, [128, FF], F32, pw) for i in range(NWB)]
        wst_t = k.tls("wst", NWB, dma=True)
        wbf = [sb("wbf%d" % i, [128, FF], BF16, pw) for i in range(NWB)]
        wbf_t = k.tls("wbf", NWB, dma=True)
        jobs = []

        def conv(loads, ncols, dst, dst_t, zero=False):
            jobs.append((loads, ncols, dst, dst_t, zero))

        v3 = lambda t, o, n: t[:, o:o + n * 128].rearrange("p (a b) -> p a b", b=128)
        for f in range(2):
            w1v = fw1_d[f].rearrange("(kc p) n -> p kc n", p=128)
            w3v = fw3_d[f].rearrange("(kc p) n -> p kc n", p=128)
            w2v = fw2_d[f].rearrange("(jc p) n -> p jc n", p=128)
            for j in range(NJ):
                conv([(lambda t: v3(t, 0, 8), w1v[:, :, j * 128:(j + 1) * 128]),
                      (lambda t: v3(t, 1024, 8), w3v[:, :, j * 128:(j + 1) * 128])], 2048, w13s[f][j, :, :], drt("w13s%d" % f))
            for m in range(8):
                conv([(lambda t: v3(t, 0, NJ), w2v[:, :, m * 128:(m + 1) * 128])], FF, w2s[f][m, :, :], drt("w2s%d" % f))
        winv = w_in_d.rearrange("(kc p) n -> p kc n", p=128)
        for q in range(NZ):
            if q < NZ - 1:
                conv([(lambda t: v3(t, 0, 8), winv[:, :, q * 128:(q + 1) * 128])], 1024, wins[q, :, :], drt("wins"))
            else:
                conv([(lambda t: v3(t, 0, 8)[:, :, 0:32], winv[:, :, q * 128:q * 128 + 32])], 1024, wins[q, :, :], drt("wins"), zero=True)
        woutv = w_out_d.rearrange("(kc p) n -> p kc n", p=128)
        for m in range(8):
            conv([(lambda t: v3(t, 0, 8), woutv[:, :, m * 128:(m + 1) * 128])], 1024, wouts[m, :, :], drt("wouts"))
        engs3 = [(DVE, dve), (POOL, pool), (ACT, act)]

        def issue_loads(i):
            loads, ncols, dst, dst_t, zero = jobs[i]
            b = i % NWB
            if zero:
                k.op(DVE, [], [wst_t[b]], lambda: dve.memset(wst[b][:, 0:ncols], 0.0))
            for (o_ap, i_ap) in loads:
                k.dma(o_ap(wst[b]), i_ap, [], [wst_t[b]], wst_t[b])

        PRE = 2
        for i in range(min(PRE, len(jobs))):
            issue_loads(i)
        for i in range(len(jobs)):
            if i + PRE < len(jobs):
                issue_loads(i + PRE)
            loads, ncols, dst, dst_t, zero = jobs[i]
            b = i % NWB
            E, e = engs3[i % 3]
            if E is ACT:
                k.op(E, [wst_t[b]], [wbf_t[b]], lambda: act.activation(out=wbf[b][:, 0:ncols], in_=wst[b][:, 0:ncols], func=AF.Copy))
            else:
                k.op(E, [wst_t[b]], [wbf_t[b]], lambda: e.tensor_copy(out=wbf[b][:, 0:ncols], in_=wst[b][:, 0:ncols]))
            k.dma(dst, wbf[b][:, 0:ncols], [wbf_t[b]], [dst_t], wbf_t[b])
        k.barrier()

    def rsqrt_from_ps(pst, psa, epscol, out_ap, sd_ap, sd_t, out_t):
        k.op(ACT, [pst, cns_t], [sd_t], lambda: act.activation(out=sd_ap, in_=psa, func=AF.Ln, bias=epsb[:, epscol:epscol + 1], scale=1.0))
        k.op(ACT, [sd_t], [out_t], lambda: act.activation(out=out_ap, in_=sd_ap, func=AF.Exp, scale=-0.5))

    class FB:
        pass

    def alloc_ffn(st, with_xstage):
        B = FB()
        B.xfm = [sb("xfm%d" % i, [128, 8, TT], F32, st) for i in range(2)]
        B.xfm_t = [[k.tl("xfm%d_%d" % (i, j), dma=(j == 0)) for j in range(8)] for i in range(2)]
        if with_xstage:
            B.xst = [sb("xst0", [128, 4, D], F32, st)] * 2
            B.xst_t = [k.tl("xst", dma=True)] * 2
        B.sq = sb("sq", [128, 8, TT], BF16, st); B.sq_t = k.tl("sq"); B.sq2_t = k.tl("sq2")
        B.h = sb("h", [128, 8, TT], BF16, st); B.h_t = k.tls("h", 8)
        B.g = sb("g", [128, NJ, TT], BF16, st); B.g_t = k.tls("g", NJ)
        B.osb = sb("osb", [128, 8, TT], F32, st); B.osb_t = k.tls("osb", 8)
        B.tmp = [sb("ftmp%d" % i, [128, TT], F32, st) for i in range(2)]; B.tmp_t = k.tls("ftmp", 2)
        B.sa = [sb("fsa%d" % i, [128, TT], F32, st) for i in range(2)]; B.sa_t = k.tls("fsa", 2)
        B.sd = sb("fsd", [128, TT], F32, st); B.sd_t = k.tl("fsd")
        B.rstd = sb("rstd", [128, TT], F32, st); B.rstd_t = k.tl("rstd")
        B.w13 = [sb("w13b%d" % i, [128, 2, 8, 128], BF16, st) for i in range(5)]; B.w13_t = k.tls("w13b", 5, dma=True)
        B.w2 = [sb("w2b%d" % i, [128, NJ, 128], BF16, st) for i in range(3)]; B.w2_t = k.tls("w2b", 3, dma=True)
        B.n13 = 0
        B.n2 = 0
        B.q13pre = None
        B.q2 = []
        B.W = TT
        return B

    def pre_norm(B, xfm, xfm_t, s, kind):
        k.op(POOL, xfm_t[0:4], [B.sq_t], lambda: pool.tensor_tensor(out=B.sq[:, 0:4, :], in0=xfm[:, 0:4, :], in1=xfm[:, 0:4, :], op=ALU.mult))
        k.op(DVE, xfm_t[4:8], [B.sq2_t], lambda: dve.tensor_tensor(out=B.sq[:, 4:8, :], in0=xfm[:, 4:8, :], in1=xfm[:, 4:8, :], op=ALU.mult))
        k.grp(PE, [B.sq_t, B.sq2_t, onesm_t], [ps_t[6]], [mm(ps[6][:, 0:B.W], onesm[:, :], B.sq[:, kc, :], kc == 0, kc == 7) for kc in range(8)])
        rsqrt_from_ps(ps_t[6], ps[6][:, 0:B.W], 0, B.rstd[:, :], B.sd[:, :], B.sd_t, B.rstd_t)
        for kc in range(8):
            tb = kc % 2
            k.op(DVE, [xfm_t[kc], B.rstd_t, sc_t], [B.tmp_t[tb]], lambda kc=kc, tb=tb: dve.scalar_tensor_tensor(
                out=B.tmp[tb][:, :], in0=xfm[:, kc, :], scalar=sc[:, s, kind, kc:kc + 1], in1=B.rstd[:, :], op0=ALU.mult, op1=ALU.mult))
            k.op(ACT, [B.tmp_t[tb], sc_t], [B.h_t[kc]], lambda kc=kc, tb=tb: act.activation(
                out=B.h[:, kc, :], in_=B.tmp[tb][:, :], func=AF.Identity, bias=sc[:, s, kind + 1, kc:kc + 1], scale=1.0))

    def post_norm_residual(B, xfm, xfm_t, s, kind):
        k.op(POOL, B.osb_t[0:4], [B.sq_t], lambda: pool.tensor_tensor(out=B.sq[:, 0:4, :], in0=B.osb[:, 0:4, :], in1=B.osb[:, 0:4, :], op=ALU.mult))
        k.op(DVE, B.osb_t[4:8], [B.sq2_t], lambda: dve.tensor_tensor(out=B.sq[:, 4:8, :], in0=B.osb[:, 4:8, :], in1=B.osb[:, 4:8, :], op=ALU.mult))
        k.grp(PE, [B.sq_t, B.sq2_t, onesm_t], [ps_t[6]], [mm(ps[6][:, 0:B.W], onesm[:, :], B.sq[:, kc, :], kc == 0, kc == 7) for kc in range(8)])
        rsqrt_from_ps(ps_t[6], ps[6][:, 0:B.W], 0, B.rstd[:, :], B.sd[:, :], B.sd_t, B.rstd_t)
        for kc in range(8):
            tb = kc % 2
            k.op(DVE, [B.osb_t[kc], B.rstd_t, sc_t], [B.tmp_t[tb]], lambda kc=kc, tb=tb: dve.scalar_tensor_tensor(
                out=B.tmp[tb][:, :], in0=B.osb[:, kc, :], scalar=sc[:, s, kind + 2, kc:kc + 1], in1=B.rstd[:, :], op0=ALU.mult, op1=ALU.mult))
            k.op(POOL, [B.tmp_t[tb], xfm_t[kc]], [xfm_t[kc]], lambda kc=kc, tb=tb: pool.tensor_tensor(
                out=xfm[:, kc, :], in0=xfm[:, kc, :], in1=B.tmp[tb][:, :], op=ALU.add))

    def _ld13(B, f, j):
        b = B.n13 % 5
        B.n13 += 1
        k.dma(B.w13[b][:, :, :, :].rearrange("p a b c -> p (a b c)"), w13s[f][j, :, :], [drt("w13s%d" % f)], [B.w13_t[b]], B.w13_t[b])
        return b

    def _ld2(B, f, m):
        b = B.n2 % 3
        B.n2 += 1
        k.dma(B.w2[b][:, :, :].rearrange("p a b -> p (a b)"), w2s[f][m, :, :], [drt("w2s%d" % f)], [B.w2_t[b]], B.w2_t[b])
        return b

    def ffn_ab(B, f):
        q13 = B.q13pre if B.q13pre else [_ld13(B, f, 0), _ld13(B, f, 1), _ld13(B, f, 2), _ld13(B, f, 3)]
        B.q13pre = None
        B.q2 = []
        for j in range(NJ):
            if j + 4 < NJ:
                q13.append(_ld13(B, f, j + 4))
            if j == NJ - 2:
                B.q2.append(_ld2(B, f, 0))
            if j == NJ - 1:
                B.q2.append(_ld2(B, f, 1))
            wb = q13[j]
            pa, pb = (j % 2) * 2, (j % 2) * 2 + 1
            k.grp(PE, B.h_t + [B.w13_t[wb]], [ps_t[pa]], [mm(ps[pa][:, :], B.w13[wb][:, 0, kc, :], B.h[:, kc, :], kc == 0, kc == 7) for kc in range(8)])
            k.grp(PE, B.h_t + [B.w13_t[wb]], [ps_t[pb]], [mm(ps[pb][:, :], B.w13[wb][:, 1, kc, :], B.h[:, kc, :], kc == 0, kc == 7) for kc in range(8)])
            sbi = j % 2
            k.op(ACT, [ps_t[pa]], [B.sa_t[sbi]], lambda pa=pa, sbi=sbi: act.activation(out=B.sa[sbi][:, :], in_=ps[pa][:, :], func=AF.Silu))
            k.op(DVE, [B.sa_t[sbi], ps_t[pb]], [B.g_t[j]], lambda pb=pb, sbi=sbi, j=j: dve.tensor_tensor(out=B.g[:, j, :], in0=B.sa[sbi][:, :], in1=ps[pb][:, :], op=ALU.mult))

    def ffn_o(B, f, prefetch_next):
        q2 = B.q2
        for m in range(8):
            if m + 2 < 8:
                q2.append(_ld2(B, f, m + 2))
            if m == 6 and prefetch_next:
                B.q13pre = [_ld13(B, f, 0), _ld13(B, f, 1), _ld13(B, f, 2), _ld13(B, f, 3)]
            wb = q2[m]
            po = 4 + (m % 2)
            k.grp(PE, B.g_t + [B.w2_t[wb]], [ps_t[po]], [mm(ps[po][:, :], B.w2[wb][:, jc, :], B.g[:, jc, :], jc == 0, jc == NJ - 1) for jc in range(NJ)])
            k.op(ACT, [ps_t[po]], [B.osb_t[m]], lambda po=po, m=m: act.activation(out=B.osb[:, m, :], in_=ps[po][:, :], func=AF.Copy))

    def load_x_tokmajor(B, s, t0, xb):
        sbi = xb
        k.dma(B.xst[sbi][:, :, :], x_in[s][t0:t0 + TT, :].rearrange("(b p) d -> p b d", p=128), [], [B.xst_t[sbi]], B.xst_t[sbi])

    def transpose_in(B, xb):
        for kc in range(8):
            pb = 4 + (kc % 2)
            k.grp(PE, [B.xst_t[xb], ident_t], [ps_t[pb]], [lambda kc=kc, pb=pb, q=q: pe.transpose(ps[pb][:, q * 128:(q + 1) * 128], B.xst[xb][:, q, kc * 128:(kc + 1) * 128], ident[:, :]) for q in range(4)])
            if kc % 2 == 0:
                k.op(ACT, [ps_t[pb]], [B.xfm_t[xb][kc]], lambda kc=kc, pb=pb: act.activation(out=B.xfm[xb][:, kc, :], in_=ps[pb][:, :], func=AF.Copy))
            else:
                k.op(DVE, [ps_t[pb]], [B.xfm_t[xb][kc]], lambda kc=kc, pb=pb: dve.tensor_copy(out=B.xfm[xb][:, kc, :], in_=ps[pb][:, :]))

    tiles = [(s, t0) for s in range(2) for t0 in range(0, seqT[s], TT)]

    with ExitStack() as pa_:
        PFX[0] = "A_"
        B = alloc_ffn(pa_, True)
        win = sb("winb", [128, NZ, 8, 128], BF16, pa_)
        win_t = k.tl("winb", const=True, dma=True)
        for q in range(NZ):
            k.dma(win[:, q, :, :].rearrange("p a b -> p (a b)"), wins[q, :, :], [drt("wins")], [win_t], win_t)
        zst = [sb("zst%d" % i, [128, TT], BF16, pa_) for i in range(3)]
        zst_t = k.tls("zst", 3, dma=True)
        print("pass A sbuf remaining", nc.sbuf_bytes_remaining, flush=True)
        h1, h1_t = B.h, B.h_t
        h2 = sb("h2", [128, 8, TT], BF16, pa_)
        h2_t = k.tls("h2_", 8)
        load_x_tokmajor(B, tiles[0][0], tiles[0][1], 0)
        transpose_in(B, 0)
        if len(tiles) > 1:
            load_x_tokmajor(B, tiles[1][0], tiles[1][1], 1)
        pre_norm(B, B.xfm[0], B.xfm_t[0], tiles[0][0], 0)
        def z_phase(s, t0):
            for q in range(NZ):
                pz = q % 4
                zb = q % 3
                k.grp(PE, h2_t + [win_t], [ps_t[pz]], [mm(ps[pz][:, :], win[:, q, kc, :], h2[:, kc, :], kc == 0, kc == 7) for kc in range(8)])
                if q % 2 == 0:
                    k.op(ACT, [ps_t[pz]], [zst_t[zb]], lambda pz=pz, zb=zb: act.activation(out=zst[zb][:, :], in_=ps[pz][:, :], func=AF.Copy))
                else:
                    k.op(DVE, [ps_t[pz]], [zst_t[zb]], lambda pz=pz, zb=zb: dve.tensor_copy(out=zst[zb][:, :], in_=ps[pz][:, :]))
                k.dma(zT[s][q * 128:(q + 1) * 128, t0:t0 + TT], zst[zb][:, :], [zst_t[zb]], [drt("zT%d" % s)], zst_t[zb])

        for ti, (s, t0) in enumerate(tiles):
            xb = ti % 2
            xfm, xfm_t = B.xfm[xb], B.xfm_t[xb]
            last = ti + 1 == len(tiles)
            ffn_ab(B, 0)
            if ti > 0:
                z_phase(*tiles[ti - 1])
            if not last:
                transpose_in(B, 1 - xb)
                if ti + 2 < len(tiles):
                    load_x_tokmajor(B, tiles[ti + 2][0], tiles[ti + 2][1], xb)
                pre_norm(B, B.xfm[1 - xb], B.xfm_t[1 - xb], tiles[ti + 1][0], 0)
            ffn_o(B, 0, not last)
            post_norm_residual(B, xfm, xfm_t, s, 0)
            k.dma(x1T[s].rearrange("(kc p) t -> p kc t", p=128)[:, :, t0:t0 + TT], xfm[:, :, :], xfm_t, [drt("x1T%d" % s)], xfm_t[0])
            B.h, B.h_t = h2, h2_t
            pre_norm(B, xfm, xfm_t, s, 3)
            B.h, B.h_t = h1, h1_t
        z_phase(*tiles[-1])
        k.barrier()
    if stage == "A":
        return

    ST = 256
    NCK = ST // CH
    ZR = 16

    def scan_pass(d, with_mix, stage_stop=None):
        with ExitStack() as st:
            PFX[0] = "C_" if with_mix else "B_"
            cm = sb("cm", [128, 640], F32, st)
            cmT4 = sb("cmT4", [128, 4, 128], F32, st)
            w2b = sb("w2b", [128, 512], BF16, st)
            a2b = sb("a2b", [128, 512], BF16, st)
            ones128 = sb("ones128", [128, 128], F32, st)
            cst = sb("cst", [128, 512], F32, st)
            cst_t = k.tl("cst", dma=True)
            pc_t = k.tl("pconst", const=True, dma=True)
            k.dma(cm[:, :], cmask_d[d, :, :], [], [pc_t], pc_t)
            for q in range(4):
                k.dma(cmT4[:, q, :], cmask_d[d, :, 512:640], [], [pc_t], pc_t)
            k.op(DVE, [], [pc_t], lambda: dve.memset(ones128[:, :], 1.0))
            k.dma(cst[:, :], w2r_d[:, :], [], [cst_t], cst_t)
            k.op(DVE, [cst_t], [pc_t], lambda: dve.tensor_copy(out=w2b[:, :], in_=cst[:, :]))
            k.dma(cst[:, :], a2r_d[:, :], [cst_t], [cst_t], cst_t)
            k.op(DVE, [cst_t], [pc_t], lambda: dve.tensor_copy(out=a2b[:, :], in_=cst[:, :]))
            if with_mix:
                g2b = sb("g2b", [128, 2, 512], BF16, st)
                pwb = sb("pwb", [128, 4, 128], BF16, st)
                woutb = sb("woutb", [128, 8, 8, 128], BF16, st)
                pcr = sb("pcr", [128, 2, 4, 8], F32, st)
                for q in range(2):
                    k.dma(cst[:, :], g2p_d[q * 128:(q + 1) * 128, :], [cst_t], [cst_t], cst_t)
                    k.op(DVE, [cst_t], [pc_t], lambda q=q: dve.tensor_copy(out=g2b[:, q, :], in_=cst[:, :]))
                k.dma(cst[:, :].rearrange("p (a b) -> p a b", b=128), pool_w_d.rearrange("g c d -> c g d"), [cst_t], [cst_t], cst_t)
                k.op(DVE, [cst_t], [pc_t], lambda: dve.tensor_copy(out=pwb[:, :, :].rearrange("p a b -> p (a b)"), in_=cst[:, :]))
                for m in range(8):
                    k.dma(woutb[:, m, :, :].rearrange("p a b -> p (a b)"), wouts[m, :, :], [drt("wouts")], [pc_t], pc_t)
                k.dma(pcr[:, :, :, :], pcorr_d[:, :, :, :], [], [pc_t], pc_t)
            k.barrier()
            zl = sb("zl", [128, ZR, ST + 2], BF16, st); zl_t = k.tl("zl", dma=True)
            zs = sb("zs", [128, ZR, ST], F32, st); zs_t = k.tls("zs", ZR)
            tA = [sb("stA%d" % i, [128, ST], F32, st) for i in range(2)]; tA_t = k.tls("stA", 2)
            tB = [sb("stB%d" % i, [128, ST], F32, st) for i in range(2)]; tB_t = k.tls("stB", 2)
            twb = sb("twb", [128, ST], BF16, st); twb_t = k.tl("twb")
            zab = sb("zab", [128, ST], BF16, st); zab_t = k.tl("zab")
            sg = sb("sg", [128, 4, ST], F32, st); sg_t = k.tl("sg", dma=with_mix)
            aa = sb("aa", [128, 4, ST], F32, st); aa_t = k.tl("aa", dma=with_mix)
            cumee = sb("cumee", [128, 2, 4, ST], F32, st)
            cum = cumee[:, 0, :, :]; ee = cumee[:, 1, :, :]
            cum_t = k.tl("cumee", dma=with_mix); ee_t = cum_t
            tot = sb("tot", [128, 3, 4 * NCK], F32, st); tot_t = k.tl("tot")
            Wbuf = sb("Wbuf", [128, 3, 4, ST], F32, st)
            winc = Wbuf[:, 0, :, :]; winv = Wbuf[:, 1, :, :]; wexc = Wbuf[:, 2, :, :]
            wx_t = k.tl("wx")
            kkr_f = sb("kkr", [128, 4, ST + 16], F32, st); kkr = kkr_f[:, :, 0:ST]; kkr_t = k.tl("kkr")
            sqk = sb("sqk", [128, 4, ST], BF16, st); sqk_t = k.tl("sqk")
            sdk = sb("sdk", [128, 4, ST], F32, st); sdk_t = k.tl("sdk")
            kk_f = sb("kk", [128, 4, ST + 16], F32, st); kk = kk_f[:, :, 0:ST]; kk_t = k.tl("kk")
            kd_f = sb("kd", [128, 4, ST + 16], F32, st); kd = kd_f[:, :, 0:ST]; kd_t = k.tl("kd")
            t1 = sb("t1", [128, 4, ST], F32, st); t1_t = k.tl("t1")
            bb_f = sb("bbq", [128, 4, ST + 16], F32, st); bb = bb_f[:, :, 0:ST]; bb_t = k.tl("bbq", dma=with_mix)
            rkb = sb("rkb", [128, 4, ST], BF16, st); rkb_t = k.tl("rkb")
            bon = sb("bon", [128, 4, ST], F32, st); bon_t = k.tl("bon", dma=True)
            KR = sb("KR", [128, 4, NCK, 2, CH], BF16, st); KR_t = k.tl("KR")
            ktbt = sb("ktbt", [128, 2, 4, ST], BF16, st)
            kt = ktbt[:, 0, :, :]; bt = ktbt[:, 1, :, :]
            kt_t = k.tl("ktbt"); bt_t = kt_t
            vb = sb("vb", [128, 4, ST], BF16, st); vb_t = k.tl("vb")
            ysb = sb("ysb", [128, 4, ST], F32, st); ysb_t = k.tl("ysb", dma=True)
            A4c = [sb("A4_%d" % c, [128, 8, 512], BF16, st) for c in range(NCK)]; A4c_t = [k.tls("A4_%d_" % c, 8) for c in range(NCK)]
            Nbc = [[sb("Nb%d_%d" % (c, i), [128, 8, CH], BF16, st) for i in range(2)] for c in range(NCK)]
            Nbc_t = [[k.tls("Nb%d_%d_" % (c, i), 2) for i in range(2)] for c in range(NCK)]
            NTbc = [[sb("NTb%d_%d" % (c, i), [128, 8, CH], BF16, st) for i in range(2)] for c in range(NCK)]
            NTbc_t = [[k.tls("NTb%d_%d_" % (c, i), 2) for i in range(2)] for c in range(NCK)]
            Zbc = [[sb("Zb%d_%d" % (c, i), [128, 8, CH], BF16, st) for i in range(2)] for c in range(NCK)]
            Zbc_t = [[k.tls("Zb%d_%d_" % (c, i), 2) for i in range(2)] for c in range(NCK)]
            VtZc = [sb("VtZ%d" % c, [128, 8, CH], BF16, st) for c in range(NCK)]; VtZc_t = k.tls("VtZ", NCK)
            UnZ = sb("UnZ", [128, 8, CH], BF16, st); UnZ_t = k.tl("UnZ")
            Ktokc = [sb("Ktok%d" % c, [128, 4, CH], BF16, st) for c in range(NCK)]
            Btokc = [sb("Btok%d" % c, [128, 4, CH], BF16, st) for c in range(NCK)]; KBc_t = k.tls("KBtok", NCK)
            X1sb = sb("X1sb", [128, 4, CH], BF16, st); X1_t = k.tl("X1sb")
            Gbd = [sb("Gbd%d" % i, [128, 4, CH], BF16, st) for i in range(2)]; Gbd_t = k.tls("Gbd", 2)
            psb6 = ps[6][:, :].bitcast(BF16)
            psb5 = ps[5][:, :].bitcast(BF16)
            for c in range(NCK):
                k.op(POOL, [], [VtZc_t[c]], lambda c=c: pool.memset(VtZc[c][:, :, :], 0.0))
            k.op(POOL, [], [UnZ_t], lambda: pool.memset(UnZ[:, :, :], 0.0))
            if with_mix:
                ybl = sg; ybl_t = sg_t
                bbl = aa; bbl_t = aa_t
                xfm = cumee[:, :, :, :].rearrange("p a c t -> p (a c) t"); xfm_t = [cum_t] * 8
                zp = sb("zp", [128, 4, ST + 16], BF16, st); zp_t = k.tl("zp", dma=True)
                s2 = kkr_f; s4 = kk_f; s8 = kd_f
                s16 = sb("s16", [128, 1, ST + 16], F32, st)
                sp_t = k.tl("spool")
                spl = [kkr_t, kk_t, kd_t, sp_t]
                pp = rkb; pp_t = rkb_t
                mix = KR[:, :, :, :, :].rearrange("p c i a t -> p (c i a t)").rearrange("p (k t) -> p k t", t=ST); mix_t = [KR_t] * 8
                ybf = sqk; ybf_t = sqk_t
                yc = t1; yc_t = t1_t
                sgz = sb("sgz", [128, 2, ST], BF16, st); sgz_t = k.tl("sgz")
                Bc = FB()
                Bc.sq = ktbt[:, :, :, :].rearrange("p a c t -> p (a c) t"); Bc.sq_t = kt_t; Bc.sq2_t = kt_t
                Bc.osb = Wbuf[:, 0:2, :, :].rearrange("p a c t -> p (a c) t"); Bc.osb_t = [wx_t] * 8
                Bc.tmp = [sb("ctmp%d" % i, [128, ST], F32, st) for i in range(2)]; Bc.tmp_t = k.tls("ctmp", 2)
                Bc.sd = sb("csd", [128, ST], F32, st); Bc.sd_t = k.tl("csd")
                Bc.rstd = sb("crstd", [128, ST], F32, st); Bc.rstd_t = k.tl("crstd")
                Bc.W = ST

                def mix_tail(s, t0, T, zrows):
                    yrows = lambda tns: tns.rearrange("(c p) t -> p c t", p=128)[:, :, t0:t0 + ST]
                    k.dma(ybl[:, :, :], yrows(ybT[s]), [drt("ybT%d" % s)], [ybl_t], ybl_t)
                    k.dma(bbl[:, :, :], yrows(bbT[s]), [drt("bbT%d" % s)], [bbl_t], bbl_t)
                    k.dma(xfm[:, :, :], x1T[s].rearrange("(kc p) t -> p kc t", p=128)[:, :, t0:t0 + ST], [drt("x1T%d" % s)], xfm_t, xfm_t[0])
                    lo, hi = max(t0 - 8, 0), min(t0 + ST + 8, T)
                    off = lo - (t0 - 8)
                    if t0 == 0:
                        k.op(POOL, [], [zp_t], lambda: pool.memset(zp[:, :, 0:8], 0.0))
                    if t0 + ST == T:
                        k.op(POOL, [], [zp_t], lambda: pool.memset(zp[:, :, ST + 8:ST + 16], 0.0))
                    k.dma(zp[:, :, off:off + hi - lo], zrows[:, 0:4, lo:hi], [drt("zT%d" % s)], [zp_t], zp_t)
                    dbg("dbg_yf", ysb[:, :, :], [128, 4, ST], F32, [ysb_t])
                    dbg("dbg_ybl", ybl[:, :, :], [128, 4, ST], F32, [ybl_t])
                    k.op(POOL, [ysb_t, ybl_t], [ybl_t], lambda: pool.tensor_tensor(out=ybl[:, :, :], in0=ysb[:, :, :], in1=ybl[:, :, :], op=ALU.add))
                    k.op(POOL, [bon_t, bbl_t], [bbl_t], lambda: pool.tensor_tensor(out=bbl[:, :, :], in0=bon[:, :, :], in1=bbl[:, :, :], op=ALU.add))
                    k.op(ACT, [ybl_t], [ybf_t], lambda: act.activation(out=ybf[:, :, :], in_=ybl[:, :, :], func=AF.Copy))
                    for half in range(2):
                        pbk = half
                        k.grp(PE, [ybf_t, bonesb_t], [ps_t[pbk]], [mm(ps[pbk][:, c2 * ST:(c2 + 1) * ST], bavgb[:, :], ybf[:, half * 2 + c2, :], True, True) for c2 in range(2)])
                        k.op(DVE, [ps_t[pbk], ybl_t], [yc_t], lambda half=half, pbk=pbk: dve.tensor_tensor(
                            out=yc[:, half * 2:half * 2 + 2, :], in0=ybl[:, half * 2:half * 2 + 2, :], in1=ps[pbk][:, :].rearrange("p (a b) -> p a b", b=ST), op=ALU.subtract))
                    k.op(POOL, [yc_t], [ybf_t], lambda: pool.tensor_tensor(out=ybf[:, :, :], in0=yc[:, :, :], in1=yc[:, :, :], op=ALU.mult))
                    for half in range(2):
                        pbk = 2 + half
                        k.grp(PE, [ybf_t, bonesb_t], [ps_t[pbk]], [mm(ps[pbk][:, c2 * ST:(c2 + 1) * ST], bavgb[:, :], ybf[:, half * 2 + c2, :], True, True) for c2 in range(2)])
                        k.op(ACT, [ps_t[pbk], cns_t], [sdk_t], lambda half=half, pbk=pbk: act.activation(out=sdk[:, half * 2:half * 2 + 2, :].rearrange("p a b -> p (a b)"), in_=ps[pbk][:, :], func=AF.Ln, bias=epsb[:, 1:2], scale=1.0))
                    k.op(ACT, [sdk_t], [sdk_t], lambda: act.activation(out=sdk[:, :, :], in_=sdk[:, :, :], func=AF.Exp, scale=-0.5))
                    k.op(POOL, [yc_t, sdk_t], [yc_t], lambda: pool.tensor_tensor(out=yc[:, :, :], in0=yc[:, :, :], in1=sdk[:, :, :], op=ALU.mult))
                    for cc in range(4):
                        k.op(DVE, [yc_t, pk_t], [yc_t], lambda cc=cc: dve.tensor_scalar(out=yc[:, cc, :], in0=yc[:, cc, :], scalar1=pk2T[:, LG + cc:LG + cc + 1], scalar2=pk2T[:, LB + cc:LB + cc + 1], op0=ALU.mult, op1=ALU.add))
                    k.op(POOL, [yc_t, bbl_t], [yc_t], lambda: pool.tensor_tensor(out=yc[:, :, :], in0=yc[:, :, :], in1=bbl[:, :, :], op=ALU.add))
                    for q in range(2):
                        k.op(ACT, [zs_t[14 + q]], [sgz_t], lambda q=q: act.activation(out=sgz[:, q, :], in_=zs[:, 14 + q, :], func=AF.Sigmoid))
                    for half in range(2):
                        pbk = 4 + half if half == 0 else 0
                        fns = []
                        for c2 in range(2):
                            cc = half * 2 + c2
                            fns.append(mm(ps[pbk][:, c2 * ST:(c2 + 1) * ST], g2b[:, 0, cc * 128:(cc + 1) * 128], sgz[:, 0, :], True, False))
                            fns.append(mm(ps[pbk][:, c2 * ST:(c2 + 1) * ST], g2b[:, 1, cc * 128:(cc + 1) * 128], sgz[:, 1, :], False, True))
                        k.grp(PE, [sgz_t, pc_t], [ps_t[pbk]], fns)
                        k.op(DVE, [ps_t[pbk], yc_t], mix_t[4 + half * 2:6 + half * 2], lambda half=half, pbk=pbk: dve.tensor_tensor(
                            out=mix[:, 4 + half * 2:6 + half * 2, :], in0=yc[:, half * 2:half * 2 + 2, :], in1=ps[pbk][:, :].rearrange("p (a b) -> p a b", b=ST), op=ALU.mult))
                    L_ = ST + 16
                    k.op(POOL, [zp_t], spl, lambda: pool.tensor_tensor(out=s2[:, :, 0:L_ - 1], in0=zp[:, :, 0:L_ - 1], in1=zp[:, :, 1:L_], op=ALU.add))
                    k.op(POOL, spl, spl, lambda: pool.tensor_tensor(out=s4[:, 1:4, 0:L_ - 3], in0=s2[:, 1:4, 0:L_ - 3], in1=s2[:, 1:4, 2:L_ - 1], op=ALU.add))
                    k.op(POOL, spl, spl, lambda: pool.tensor_tensor(out=s8[:, 2:4, 0:L_ - 7], in0=s4[:, 2:4, 0:L_ - 7], in1=s4[:, 2:4, 4:L_ - 3], op=ALU.add))
                    k.op(POOL, spl, spl, lambda: pool.tensor_tensor(out=s16[:, 0, 0:L_ - 15], in0=s8[:, 3, 0:L_ - 15], in1=s8[:, 3, 8:L_ - 7], op=ALU.add))
                    wsum = [s2[:, 0, 7:7 + ST], s4[:, 1, 6:6 + ST], s8[:, 2, 4:4 + ST], s16[:, 0, 0:ST]]
                    for g in range(4):
                        if t0 == 0:
                            k.op(DVE, spl + [pc_t], spl, lambda g=g: dve.tensor_tensor(out=wsum[g][:, 0:8], in0=wsum[g][:, 0:8], in1=pcr[:, 0, g, :], op=ALU.mult))
                        if t0 + ST == T:
                            k.op(DVE, spl + [pc_t], spl, lambda g=g: dve.tensor_tensor(out=wsum[g][:, ST - 8:ST], in0=wsum[g][:, ST - 8:ST], in1=pcr[:, 1, g, :], op=ALU.mult))
                        k.op(DVE, spl + [zp_t], [pp_t], lambda g=g: dve.scalar_tensor_tensor(out=pp[:, g, :], in0=wsum[g], scalar=1.0 / (2 << g), in1=zp[:, g, 8:8 + ST], op0=ALU.mult, op1=ALU.subtract))
                    for half in range(2):
                        pbk = 1 + half
                        k.grp(PE, [pp_t, pc_t], [ps_t[pbk]], [mm(ps[pbk][:, c2 * ST:(c2 + 1) * ST], pwb[:, half * 2 + c2, :], pp[:, half * 2 + c2, :], True, True) for c2 in range(2)])
                        for c2 in range(2):
                            g = half * 2 + c2
                            k.op(ACT, [ps_t[pbk], pk_t], [mix_t[g]], lambda g=g, c2=c2, pbk=pbk: act.activation(out=mix[:, g, :], in_=ps[pbk][:, c2 * ST:(c2 + 1) * ST], func=AF.Copy, scale=pk2T[:, PSC + g:PSC + g + 1]))
                    dbg("dbg_mix", mix[:, :, :], [128, 8, ST], BF16, mix_t)
                    dbg("dbg_yc", yc[:, :, :], [128, 4, ST], F32, [yc_t])
                    for mo in range(8):
                        pbk = 3 + (mo % 2)
                        k.grp(PE, mix_t + [pc_t], [ps_t[pbk]], [mm(ps[pbk][:, 0:ST], woutb[:, mo, kc, :], mix[:, kc, :], kc == 0, kc == 7) for kc in range(8)])
                        k.op(ACT, [ps_t[pbk]], [Bc.osb_t[mo]], lambda mo=mo, pbk=pbk: act.activation(out=Bc.osb[:, mo, :], in_=ps[pbk][:, 0:ST], func=AF.Copy))
                    post_norm_residual(Bc, xfm, xfm_t, s, 3)
                    for (dv, ko) in x2_views(s, t0, t0 + ST):
                        k.dma(dv, xfm[:, ko:ko + 4, :], xfm_t + [ybl_t, bbl_t], [drt("x2T%d" % s)], xfm_t[0])

            print("scan pass sbuf remaining", nc.sbuf_bytes_remaining, flush=True)
            def load_zl(s, sti):
                T = seqT[s]
                t0 = sti * ST
                zrows = zT[s].rearrange("(j p) t -> p j t", p=128)
                lo, hi = max(t0 - 1, 0), min(t0 + ST + 1, T)
                off = lo - (t0 - 1)
                if t0 == 0:
                    k.op(POOL, [], [zl_t], lambda: pool.memset(zl[:, :, 0:1], 0.0))
                if t0 + ST == T:
                    k.op(POOL, [], [zl_t], lambda: pool.memset(zl[:, :, ST + 1:ST + 2], 0.0))
                k.dma(zl[:, :, off:off + hi - lo], zrows[:, 4:4 + ZR, lo:hi], [drt("zT%d" % s)], [zl_t], zl_t)

            work = [(s, sti) for s in range(2) for sti in (range(seqT[s] // ST) if d == 0 else range(seqT[s] // ST - 1, -1, -1))]
            load_zl(*work[0])
            gcur = 0
            for wi, (s, sti) in enumerate(work):
                T = seqT[s]
                zrows = zT[s].rearrange("(j p) t -> p j t", p=128)
                if wi == 0 or work[wi - 1][0] != s:
                    gcur = 0
                    k.op(POOL, [], [Gbd_t[0]], lambda: pool.memset(Gbd[0][:, :, :], 0.0))
                if True:
                    t0 = sti * ST
                    rows = list(range(14)) + ([14, 15] if with_mix else [])
                    for ri, j in enumerate(rows):
                        b2 = ri % 2
                        k.op(POOL, [zl_t], [tA_t[b2]], lambda j=j, b2=b2: pool.tensor_tensor(out=tA[b2][:, :], in0=zl[:, j, 0:ST], in1=zl[:, j, 2:ST + 2], op=ALU.add))
                        k.op(ACT, [zl_t, cns_t], [tB_t[b2]], lambda j=j, b2=b2: act.activation(out=tB[b2][:, :], in_=zl[:, j, 1:ST + 1], func=AF.Copy, scale=ommu[:, j:j + 1]))
                        k.op(DVE, [tA_t[b2], tB_t[b2], cns_t], [zs_t[j]], lambda j=j, b2=b2: dve.scalar_tensor_tensor(
                            out=zs[:, j, :], in0=tA[b2][:, :], scalar=hmu[:, j:j + 1], in1=tB[b2][:, :], op0=ALU.mult, op1=ALU.add))
                    if wi + 1 < len(work):
                        load_zl(*work[wi + 1])
                    R_, K_, V_ = 0, 4, 8
                    v4 = lambda t_: t_[:, :, :].rearrange("p c (i t) -> p c i t", t=CH)
                    k.op(ACT, [zs_t[12]], [twb_t], lambda: act.activation(out=twb[:, :], in_=zs[:, 12, :], func=AF.Tanh))
                    k.op(POOL, [zs_t[13]], [zab_t], lambda: pool.tensor_copy(out=zab[:, :], in_=zs[:, 13, :]))
                    dp = slice(d * 64, d * 64 + 64)
                    for half in range(2):
                        pbk = half
                        k.grp(PE, [twb_t, pc_t], [ps_t[pbk]], [mm(ps[pbk][:, c2 * ST:(c2 + 1) * ST], w2b[dp, (half * 2 + c2) * 128:(half * 2 + c2 + 1) * 128], twb[dp, :], True, True) for c2 in range(2)])
                        for c2 in range(2):
                            cc = half * 2 + c2
                            k.op(ACT, [ps_t[pbk], pk_t], [sg_t], lambda cc=cc, c2=c2, pbk=pbk: act.activation(out=sg[:, cc, :], in_=ps[pbk][:, c2 * ST:(c2 + 1) * ST], func=AF.Sigmoid, bias=W0(d * 4 + cc), scale=1.0))
                    for half in range(2):
                        pbk = 2 + half
                        k.grp(PE, [zab_t, pc_t], [ps_t[pbk]], [mm(ps[pbk][:, c2 * ST:(c2 + 1) * ST], a2b[dp, (half * 2 + c2) * 128:(half * 2 + c2 + 1) * 128], zab[dp, :], True, True) for c2 in range(2)])
                        for c2 in range(2):
                            cc = half * 2 + c2
                            k.op(ACT, [ps_t[pbk], pk_t], [aa_t], lambda cc=cc, c2=c2, pbk=pbk: act.activation(out=aa[:, cc, :], in_=ps[pbk][:, c2 * ST:(c2 + 1) * ST], func=AF.Sigmoid, bias=A0(d * 4 + cc), scale=1.0))
                    for cc in range(4):
                        for ci in range(NCK):
                            k.op(DVE, [sg_t, pc_t], [cum_t], lambda cc=cc, ci=ci: dve.tensor_tensor_scan(
                                out=cum[:, cc, ci * CH:(ci + 1) * CH], data0=ones128[:, :], data1=sg[:, cc, ci * CH:(ci + 1) * CH], initial=0.0, op0=ALU.mult, op1=ALU.add))
                    k.op(POOL, [cum_t, sg_t], [ee_t], lambda: pool.tensor_tensor(out=ee[:, :, :], in0=cum[:, :, :], in1=sg[:, :, :], op=ALU.subtract))
                    cumv = cum[:, :, :].rearrange("p c (i t) -> p c i t", t=CH)
                    totv = lambda q: tot[:, q, :].rearrange("p (c i) -> p c i", i=NCK)
                    k.op(DVE, [cum_t], [tot_t], lambda: dve.tensor_copy(out=totv(0), in_=cumv[:, :, :, CH - 1]))
                    k.op(ACT, [tot_t], [tot_t], lambda: act.activation(out=tot[:, 2, :], in_=tot[:, 0, :], func=AF.Exp, scale=-CDEC))
                    if d == 0:
                        spec = [(winc, cum, -CDEC), (winv, cum, CDEC), (wexc, ee, -CDEC)]
                    else:
                        totb = totv(0).unsqueeze(3).to_broadcast([128, 4, NCK, CH])
                        k.op(DVE, [ee_t, tot_t], [kk_t], lambda: dve.tensor_tensor(out=v4(kk), in0=v4(ee), in1=totb, op=ALU.subtract))
                        k.op(POOL, [cum_t, tot_t], [kd_t], lambda: pool.tensor_tensor(out=v4(kd), in0=v4(cum), in1=totb, op=ALU.subtract))
                        spec = [(winc, kk, CDEC), (winv, kk, -CDEC), (wexc, kd, CDEC)]
                    for (o_, i_, sc_) in spec:
                        k.op(ACT, [cum_t, ee_t, kk_t, kd_t], [wx_t], lambda o_=o_, i_=i_, sc_=sc_: act.activation(out=o_[:, :, :], in_=i_[:, :, :], func=AF.Exp, scale=sc_))
                    bc4 = lambda c0: pk2T[:, c0:c0 + 4].unsqueeze(2).to_broadcast([128, 4, ST])
                    k.op(DVE, zs_t[K_:K_ + 4] + [pk_t], [kkr_t], lambda: dve.tensor_tensor(out=kkr[:, :, :], in0=zs[:, K_:K_ + 4, :], in1=bc4(KK), op=ALU.mult))
                    k.op(POOL, [kkr_t], [sqk_t], lambda: pool.tensor_tensor(out=sqk[:, :, :], in0=kkr[:, :, :], in1=kkr[:, :, :], op=ALU.mult))
                    for half in range(2):
                        pbk = 4 + half if half == 0 else 0
                        k.grp(PE, [sqk_t, bonesb_t], [ps_t[pbk]], [mm(ps[pbk][:, c2 * ST:(c2 + 1) * ST], bonesb[:, :], sqk[:, half * 2 + c2, :], True, True) for c2 in range(2)])
                        k.op(ACT, [ps_t[pbk], cns_t], [sdk_t], lambda half=half, pbk=pbk: act.activation(out=sdk[:, half * 2:half * 2 + 2, :].rearrange("p a b -> p (a b)"), in_=ps[pbk][:, :], func=AF.Ln, bias=epsb[:, 2:3], scale=1.0))
                    k.op(ACT, [sdk_t], [sdk_t], lambda: act.activation(out=sdk[:, :, :], in_=sdk[:, :, :], func=AF.Exp, scale=-0.5))
                    k.op(POOL, [kkr_t, sdk_t], [kk_t], lambda: pool.tensor_tensor(out=kk[:, :, :], in0=kkr[:, :, :], in1=sdk[:, :, :], op=ALU.mult))
                    k.op(DVE, [aa_t, pk_t], [t1_t], lambda: dve.scalar_tensor_tensor(out=t1[:, :, :], in0=aa[:, :, :], scalar=-1.0, in1=bc4(KA), op0=ALU.add, op1=ALU.mult))
                    k.op(DVE, [t1_t] + zs_t[K_:K_ + 4], [kd_t], lambda: dve.scalar_tensor_tensor(out=kd[:, :, :], in0=t1[:, :, :], scalar=1.0, in1=zs[:, K_:K_ + 4, :], op0=ALU.add, op1=ALU.mult))
                    k.op(POOL, [kk_t, aa_t], [bb_t], lambda: pool.tensor_tensor(out=bb[:, :, :], in0=kk[:, :, :], in1=aa[:, :, :], op=ALU.mult))
                    k.op(POOL, [kd_t] + zs_t[R_:R_ + 4], [t1_t], lambda: pool.tensor_tensor(out=t1[:, :, :], in0=kd[:, :, :], in1=zs[:, R_:R_ + 4, :], op=ALU.mult))
                    k.op(DVE, [t1_t, pk_t], [rkb_t], lambda: dve.tensor_tensor(out=rkb[:, :, :], in0=t1[:, :, :], in1=bc4(RK), op=ALU.mult))
                    for half in range(2):
                        pbk = 1 + half
                        k.grp(PE, [rkb_t, bonesb_t], [ps_t[pbk]], [mm(ps[pbk][:, c2 * ST:(c2 + 1) * ST], bonesb[:, :], rkb[:, half * 2 + c2, :], True, True) for c2 in range(2)])
                        k.op(DVE, [ps_t[pbk]] + zs_t[V_:V_ + 4], [bon_t], lambda half=half, pbk=pbk: dve.tensor_tensor(
                            out=bon[:, half * 2:half * 2 + 2, :], in0=zs[:, V_ + half * 2:V_ + half * 2 + 2, :], in1=ps[pbk][:, :].rearrange("p (a b) -> p a b", b=ST), op=ALU.mult))
                    k.op(POOL, [kk_t, wx_t], [KR_t], lambda: pool.tensor_tensor(out=KR[:, :, :, 0, :], in0=v4(kk), in1=v4(wexc), op=ALU.mult))
                    k.op(DVE, zs_t[R_:R_ + 4] + [wx_t], [KR_t], lambda: dve.tensor_tensor(out=KR[:, :, :, 1, :], in0=zs[:, R_:R_ + 4, :].rearrange("p c (i t) -> p c i t", t=CH), in1=v4(winc), op=ALU.mult))
                    k.op(POOL, [kd_t, wx_t], [kt_t], lambda: pool.tensor_tensor(out=kt[:, :, :], in0=kd[:, :, :], in1=winv[:, :, :], op=ALU.mult))
                    k.op(DVE, [bb_t, wx_t], [bt_t], lambda: dve.tensor_tensor(out=bt[:, :, :], in0=bb[:, :, :], in1=winv[:, :, :], op=ALU.mult))
                    k.op(ACT, zs_t[V_:V_ + 4], [vb_t], lambda: act.activation(out=vb[:, :, :], in_=zs[:, V_:V_ + 4, :], func=AF.Copy))
                    corder = list(range(NCK) if d == 0 else range(NCK - 1, -1, -1))
                    for ci in corder:
                        cs = slice(ci * CH, (ci + 1) * CH)
                        A4, A4_t, VtZ, VtZ_t = A4c[ci], A4c_t[ci], VtZc[ci], VtZc_t[ci]
                        Ktok, Btok, KB_t = Ktokc[ci], Btokc[ci], KBc_t[ci]
                        Nb, Nb_t, NTb, NTb_t, Zb, Zb_t = Nbc[ci], Nbc_t[ci], NTbc[ci], NTbc_t[ci], Zbc[ci], Zbc_t[ci]
                        k.grp(PE, [vb_t, kt_t, identb_t], [ps_t[6]],
                              [lambda cc=cc, cs=cs: pe.transpose(psb6[:, cc * CH:(cc + 1) * CH], vb[:, cc, cs], identb[:, 0, :]) for cc in range(4)] +
                              [lambda cc=cc, cs=cs: pe.transpose(psb6[:, (4 + cc) * CH:(5 + cc) * CH], kt[:, cc, cs], identb[:, 0, :]) for cc in range(4)])
                        k.grp(PE, [bt_t, identb_t], [ps_t[5]], [lambda cc=cc, cs=cs: pe.transpose(psb5[:, cc * CH:(cc + 1) * CH], bt[:, cc, cs], identb[:, 0, :]) for cc in range(4)])
                        VtZv = VtZ[:, :, :].rearrange("p (c e) (f v) -> p c e f v", e=2, f=2)
                        p6v = psb6[:, 0:512].rearrange("p (c e v) -> p c e v", e=2, v=64)
                        for e in range(2):
                            k.op(DVE, [ps_t[6]], [VtZ_t], lambda e=e, VtZv=VtZv: dve.tensor_copy(out=VtZv[:, :, e, e, :], in_=p6v[:, :, e, :]))
                        k.op(DVE, [ps_t[6]], [KB_t], lambda Ktok=Ktok: dve.tensor_copy(out=Ktok[:, :, :].rearrange("p a b -> p (a b)"), in_=psb6[:, 512:1024]))
                        k.op(ACT, [ps_t[5]], [KB_t], lambda Btok=Btok: act.activation(out=Btok[:, :, :].rearrange("p a b -> p (a b)"), in_=psb5[:, 0:512], func=AF.Copy))
                        for h in range(8):
                            cc, e = h // 2, h % 2
                            hp = slice(e * 64, e * 64 + 64)
                            pbk = h % 2
                            krv = KR[hp, cc, ci, :, :].rearrange("p a b -> p (a b)")
                            k.grp(PE, [kt_t, bt_t, KR_t], [ps_t[pbk]], [mm(ps[pbk][:, 0:256], kt[hp, cc, cs], krv, True, True), mm(ps[pbk][:, 256:512], bt[hp, cc, cs], krv, True, True)])
                            k.op(DVE, [ps_t[pbk], pc_t], [A4_t[h]], lambda h=h, pbk=pbk, A4=A4: dve.tensor_tensor(out=A4[:, h, :], in0=ps[pbk][:, :], in1=cm[:, 0:512], op=ALU.mult))
                        NT0v = NTb[0][:, :, :].rearrange("p (c e) t -> p c e t", e=2)
                        for e in range(2):
                            pbk = 2 + e
                            hp = slice(e * 64, e * 64 + 64)
                            k.grp(PE, [bt_t, KR_t], [ps_t[pbk]], [mm(ps[pbk][:, cc * CH:(cc + 1) * CH], KR[hp, cc, ci, 0, :], bt[hp, cc, cs], True, True) for cc in range(4)])
                            k.op(DVE, [ps_t[pbk], pc_t], NTb_t[0], lambda e=e, pbk=pbk, NT0v=NT0v: dve.tensor_tensor(
                                out=NT0v[:, :, e, :], in0=ps[pbk][:, :].rearrange("p (a b) -> p a b", b=CH), in1=cmT4[:, :, :], op=ALU.mult))
                        for g in range(2):
                            k.op(POOL, A4_t[g * 4:(g + 1) * 4], [Nb_t[0][g]], lambda g=g, Nb=Nb, A4=A4: pool.tensor_copy(out=Nb[0][:, g * 4:(g + 1) * 4, :], in_=A4[:, g * 4:(g + 1) * 4, 256:384]))
                            k.op(POOL, A4_t[g * 4:(g + 1) * 4] + [identb_t], [Zb_t[0][g]], lambda g=g, Zb=Zb, A4=A4: pool.tensor_tensor(out=Zb[0][:, g * 4:(g + 1) * 4, :], in0=identb[:, :, :], in1=A4[:, g * 4:(g + 1) * 4, 256:384], op=ALU.subtract))
                    grps = [(ci, g) for ci in corder for g in range(2)]
                    cur = 0
                    for lev in range(6):
                        nxt = 1 - cur
                        last = lev == 5
                        for gi, (ci, g) in enumerate(grps):
                            bs = 3 * (gi % 2)
                            X, Y = bs, bs + 1
                            Nb, Nb_t, NTb, NTb_t = Nbc[ci], Nbc_t[ci], NTbc[ci], NTbc_t[ci]
                            hs = range(g * 4, g * 4 + 4)
                            if not last:
                                k.grp(PE, [Nb_t[cur][g], NTb_t[cur][g]], [ps_t[X]], [mm(ps[X][:, (h % 4) * CH:(h % 4 + 1) * CH], NTb[cur][:, h, :], Nb[cur][:, h, :], True, True) for h in hs])
                            k.grp(PE, [Nb_t[cur][g], NTb_t[cur][g]], [ps_t[Y]], [mm(ps[Y][:, (h % 4) * CH:(h % 4 + 1) * CH], Nb[cur][:, h, :], NTb[cur][:, h, :], True, True) for h in hs])
                            if not last:
                                k.op(ACT, [ps_t[X]], [Nb_t[nxt][g]], lambda g=g, X=X, nxt=nxt, Nb=Nb: act.activation(out=Nb[nxt][:, g * 4:(g + 1) * 4, :].rearrange("p a b -> p (a b)"), in_=ps[X][:, :], func=AF.Copy))
                            k.op(DVE, [ps_t[Y]], [NTb_t[nxt][g]], lambda g=g, Y=Y, nxt=nxt, NTb=NTb: dve.tensor_copy(out=NTb[nxt][:, g * 4:(g + 1) * 4, :].rearrange("p a b -> p (a b)"), in_=ps[Y][:, :]))
                        for gi, (ci, g) in enumerate(grps):
                            Wk = 3 * (gi % 2) + 2
                            NTb, NTb_t, Zb, Zb_t = NTbc[ci], NTbc_t[ci], Zbc[ci], Zbc_t[ci]
                            fns = []
                            for h in range(g * 4, g * 4 + 4):
                                fns.append(mm(ps[Wk][:, (h % 4) * CH:(h % 4 + 1) * CH], NTb[nxt][:, h, :], Zb[cur][:, h, :], True, False))
                                fns.append(mm(ps[Wk][:, (h % 4) * CH:(h % 4 + 1) * CH], identb[:, 0, :], Zb[cur][:, h, :], False, True))
                            k.grp(PE, [NTb_t[nxt][g], Zb_t[cur][g], identb_t], [ps_t[Wk]], fns)
                            if gi % 2 == 0:
                                k.op(ACT, [ps_t[Wk]], [Zb_t[nxt][g]], lambda g=g, Wk=Wk, nxt=nxt, Zb=Zb: act.activation(out=Zb[nxt][:, g * 4:(g + 1) * 4, :].rearrange("p a b -> p (a b)"), in_=ps[Wk][:, :], func=AF.Copy))
                            else:
                                k.op(DVE, [ps_t[Wk]], [Zb_t[nxt][g]], lambda g=g, Wk=Wk, nxt=nxt, Zb=Zb: dve.tensor_copy(out=Zb[nxt][:, g * 4:(g + 1) * 4, :].rearrange("p a b -> p (a b)"), in_=ps[Wk][:, :]))
                        cur = nxt
                    for ci in corder:
                        cs = slice(ci * CH, (ci + 1) * CH)
                        A4, A4_t, VtZ, VtZ_t = A4c[ci], A4c_t[ci], VtZc[ci], VtZc_t[ci]
                        Ktok, Btok, KB_t = Ktokc[ci], Btokc[ci], KBc_t[ci]
                        Zf, Zf_t = Zbc[ci][cur], Zbc_t[ci][cur]
                        UnZv = UnZ[:, :, :].rearrange("p (c e) (f v) -> p c e f v", e=2, f=2)
                        G0, G0_t = Gbd[gcur], Gbd_t[gcur]
                        G1, G1_t = Gbd[1 - gcur], Gbd_t[1 - gcur]
                        fns = []
                        for cc in range(4):
                            o_ = ps[0][:, cc * CH:(cc + 1) * CH]
                            fns.append(mm(o_, KR[:, cc, ci, 0, :], G0[:, cc, :], True, False))
                            for e in range(2):
                                fns.append(mm(o_, A4[:, 2 * cc + e, 0:128], VtZ[:, 2 * cc + e, :], False, e == 1))
                        k.grp(PE, [KR_t, G0_t, VtZ_t] + A4_t, [ps_t[0]], fns)
                        k.op(ACT, [ps_t[0]], [X1_t], lambda: act.activation(out=X1sb[:, :, :].rearrange("p a b -> p (a b)"), in_=ps[0][:, :], func=AF.Copy))
                        k.grp(PE, [X1_t] + Zf_t, [ps_t[1]], [mm(ps[1][:, h * 64:(h + 1) * 64], Zf[:, h, :], X1sb[:, h // 2, (h % 2) * 64:(h % 2) * 64 + 64], True, True) for h in range(8)])
                        p1v = ps[1][:, :].rearrange("p (c e v) -> p c e v", e=2, v=64)
                        for e in range(2):
                            k.op(DVE, [ps_t[1]], [UnZ_t], lambda e=e: dve.tensor_scalar(out=UnZv[:, :, e, e, :], in0=p1v[:, :, e, :], scalar1=-1.0, scalar2=None, op0=ALU.mult))
                        fns = []
                        for cc in range(4):
                            o_ = ps[3][:, cc * CH:(cc + 1) * CH]
                            fns.append(mm(o_, identb[:, 0, :], G0[:, cc, :], True, False))
                            for e in range(2):
                                h = 2 * cc + e
                                fns.append(mm(o_, Ktok[:, cc, :], VtZ[:, h, :], False, False))
                                fns.append(mm(o_, Btok[:, cc, :], UnZ[:, h, :], False, e == 1))
                        k.grp(PE, [G0_t, VtZ_t, UnZ_t, KB_t, identb_t], [ps_t[3]], fns)
                        for cc in range(4):
                            k.op(DVE, [ps_t[3], tot_t, bones_t], [G1_t], lambda cc=cc, ci=ci, G1=G1: dve.scalar_tensor_tensor(
                                out=G1[:, cc, :], in0=ps[3][:, cc * CH:(cc + 1) * CH], scalar=tot[:, 2, cc * NCK + ci:cc * NCK + ci + 1], in1=bones[:, :], op0=ALU.mult, op1=ALU.mult))
                        fns = []
                        for cc in range(4):
                            o_ = ps[2][:, cc * CH:(cc + 1) * CH]
                            fns.append(mm(o_, G0[:, cc, :], KR[:, cc, ci, 1, :], True, False))
                            for e in range(2):
                                h = 2 * cc + e
                                fns.append(mm(o_, VtZ[:, h, :], A4[:, h, 128:256], False, False))
                                fns.append(mm(o_, UnZ[:, h, :], A4[:, h, 384:512], False, e == 1))
                        k.grp(PE, [KR_t, G0_t, VtZ_t, UnZ_t] + A4_t, [ps_t[2]], fns)
                        k.op(ACT, [ps_t[2]], [ysb_t], lambda cs=cs: act.activation(out=ysb[:, :, cs], in_=ps[2][:, :].rearrange("p (a b) -> p a b", b=CH), func=AF.Copy))
                        gcur = 1 - gcur
                    yrows = lambda tns: tns.rearrange("(c p) t -> p c t", p=128)[:, :, t0:t0 + ST]
                    if not with_mix:
                        k.dma(yrows(ybT[s]), ysb[:, :, :], [ysb_t], [drt("ybT%d" % s)], ysb_t)
                        k.dma(yrows(bbT[s]), bon[:, :, :], [bon_t], [drt("bbT%d" % s)], bon_t)
                        continue
                    mix_tail(s, t0, T, zrows)
            k.barrier()

    if stage == "B0":
        scan_pass(0, False)
        return
    scan_pass(1, False)
    if stage == "B":
        return
    scan_pass(0, True)
    if stage == "C":
        return

    with ExitStack() as pd_:
        PFX[0] = "D_"
        B = alloc_ffn(pd_, False)
        yst = [sb("yst%d" % i, [128, 4, D], F32, pd_) for i in range(2)]
        yst_t = k.tls("yst", 2, dma=True)

        def load_fm(ti, xb):
            s, t0 = tiles[ti]
            for (dv, ko) in x2_views(s, t0, t0 + TT):
                k.dma(B.xfm[xb][:, ko:ko + 4, :], dv, [drt("x2T%d" % s)], B.xfm_t[xb], B.xfm_t[xb][0])

        load_fm(0, 0)
        pre_norm(B, B.xfm[0], B.xfm_t[0], tiles[0][0], 6)
        for ti, (s, t0) in enumerate(tiles):
            xb = ti % 2
            last = ti + 1 == len(tiles)
            if not last:
                load_fm(ti + 1, 1 - xb)
            xfm, xfm_t = B.xfm[xb], B.xfm_t[xb]
            ffn_ab(B, 1)
            if not last:
                pre_norm(B, B.xfm[1 - xb], B.xfm_t[1 - xb], tiles[ti + 1][0], 6)
            ffn_o(B, 1, not last)
            post_norm_residual(B, xfm, xfm_t, s, 6)
            for q in range(4):
                for half in range(2):
                    pbk = (q * 2 + half) % 4
                    k.grp(PE, xfm_t + [ident_t], [ps_t[pbk]], [lambda c=c, q=q, half=half, pbk=pbk: pe.transpose(ps[pbk][:, c * 128:(c + 1) * 128], xfm[:, half * 4 + c, q * 128:(q + 1) * 128], ident[:, :]) for c in range(4)])
                    if half == 0:
                        k.op(ACT, [ps_t[pbk]], [yst_t[xb]], lambda q=q, half=half, pbk=pbk: act.activation(out=yst[xb][:, q, half * 512:(half + 1) * 512], in_=ps[pbk][:, :], func=AF.Copy))
                    else:
                        k.op(DVE, [ps_t[pbk]], [yst_t[xb]], lambda q=q, half=half, pbk=pbk: dve.tensor_copy(out=yst[xb][:, q, half * 512:(half + 1) * 512], in_=ps[pbk][:, :]))
            k.dma(y_out[s][t0:t0 + TT, :].rearrange("(b p) d -> p b d", p=128), yst[xb][:, :, :], [yst_t[xb]], [drt("y%d" % s)], yst_t[xb])
        k.barrier()


def host_inputs(inputs, TA, TB):
    g = lambda n: np.ascontiguousarray(np.asarray(inputs[n], np.float32)[0])
    f32 = np.float32
    shared = {}
    for n in ["ada_w", "f1_w1", "f1_w3", "f1_w2", "f2_w1", "f2_w3", "f2_w2", "w_in", "w_out", "pool_w"]:
        shared[n] = g(n)
    shared["w2r"] = g("w2").reshape(128, 512)
    shared["a2r"] = g("a2").reshape(128, 512)
    g2p = np.zeros((256, 512), f32)
    g2p[:160] = g("g2")
    shared["g2p"] = g2p
    pack1 = np.zeros((128, 128), f32)
    pack1[0:72] = g("ada_b").reshape(72, 128)
    for i, n in enumerate(["n1_pre", "n1_post", "nm_pre", "nm_post", "n2_pre", "n2_post"]):
        pack1[72 + 8 * i:80 + 8 * i] = g(n).reshape(8, 128)
    shared["pack1"] = pack1
    pack2 = np.zeros((128, 128), f32)
    pack2[0:8] = g("w0").reshape(8, 128)
    pack2[8:16] = g("a0").reshape(8, 128)
    for i, n in enumerate(["k_k", "k_a", "r_k", "lnx_g", "lnx_b", "pool_scale"]):
        pack2[16 + 4 * i:20 + 4 * i] = g(n).reshape(4, 128)
    mu = np.zeros((2048,), f32)
    mu[:1952] = g("shift_mu")
    pack2[40:56] = mu.reshape(16, 128)
    shared["pack2"] = pack2
    shared["ident"] = np.eye(128, dtype=f32)
    tt = np.arange(128)
    cm = np.zeros((2, 128, 640), f32)
    for d in range(2):
        strict = (tt[:, None] < tt[None, :]) if d == 0 else (tt[:, None] > tt[None, :])
        incl = (tt[:, None] <= tt[None, :]) if d == 0 else (tt[:, None] >= tt[None, :])
        cm[d, :, 0:128] = strict
        cm[d, :, 128:256] = incl
        cm[d, :, 256:384] = strict
        cm[d, :, 384:512] = incl
        cm[d, :, 512:640] = strict.T
    shared["cmask"] = cm
    shared["bones"] = np.kron(np.eye(2, dtype=f32), np.ones((64, 64), f32))
    maps = []
    xp, xs = np.asarray(inputs["x_prompt"], f32), np.asarray(inputs["x_sample"], f32)
    cp, cs = np.asarray(inputs["c_prompt"], f32), np.asarray(inputs["c_sample"], f32)
    for i in range(xp.shape[0]):
        m = dict(shared)
        m["xa"] = np.ascontiguousarray(xp[i])
        m["xb"] = np.ascontiguousarray(xs[i])
        cpk = np.zeros((128, 128), f32)
        cpk[0:8] = cp[i].reshape(8, 128)
        cpk[8:16] = cs[i].reshape(8, 128)
        m["cpack"] = cpk
        pc = np.ones((128, 2, 4, 8), f32)
        for gi, win in enumerate((2, 4, 8, 16)):
            for j in range(8):
                t = j
                cnt = (t + win // 2) - max(t - win // 2, 0)
                pc[:, 0, gi, j] = win / cnt
                t = -8 + j
                cnt = min(t + win // 2, 0) - (t - win // 2)
                pc[:, 1, gi, j] = win / cnt
        m["pcorr"] = pc
        maps.append(m)
    return maps


_NC_CACHE = {}


def kernel(**inputs):
    xp, xs = np.asarray(inputs["x_prompt"]), np.asarray(inputs["x_sample"])
    nb, TA, TB = xp.shape[0], xp.shape[1], xs.shape[1]
    key = (TA, TB)
    if key not in _NC_CACHE:
        _NC_CACHE[key] = build(TA, TB)
    nc = _NC_CACHE[key]
    maps = host_inputs(inputs, TA, TB)
    res = run_bass_kernel_spmd(nc, maps, core_ids=list(range(nb)))
    ya = np.stack([np.asarray(r["ya"], np.float32) for r in res.results], 0)
    yb = np.stack([np.asarray(r["yb"], np.float32) for r in res.results], 0)
    return (ya, yb)
```

```python
from contextlib import ExitStack
import numpy as np
import ml_dtypes
import concourse.bass as bass
import concourse.mybir as mybir
from concourse.bass_utils import run_bass_kernel_spmd

F32 = mybir.dt.float32
BF16 = mybir.dt.bfloat16
ALU = mybir.AluOpType
AF = mybir.ActivationFunctionType

D = 1024
FF = 2816
NJ = 22
NZ = 20
TT = 512
CH = 128
CDEC = float(np.exp(-0.5))
SELF_SYNC = True


class Eng:
    def __init__(self, name, eng, sem):
        self.name, self.e, self.sem, self.cnt, self.seen = name, eng, sem, 0, {}


class Tl:
    def __init__(self, name, const=False):
        self.name, self.w, self.r, self.const = name, None, {}, const
        self.dsem, self.dcnt = None, 0
        self.psum = False


class KB:
    def __init__(self, nc, es):
        self.nc, self.es = nc, es
        mk = lambda n, e: Eng(n, e, es.enter_context(nc.semaphore("sem_" + n)))
        self.PE, self.ACT, self.DVE = mk("pe", nc.tensor), mk("act", nc.scalar), mk("dve", nc.vector)
        self.POOL, self.SP = mk("pool", nc.gpsimd), mk("sp", nc.sync)
        self.nsem = 0
        self.dma_tiles = []

    def tl(self, name, const=False, dma=False):
        t = Tl(name, const)
        if dma:
            t.name = "%s#%d" % (name, self.nsem)
            t.dsem = self.es.enter_context(self.nc.semaphore("ds_%d" % self.nsem))
            self.nsem += 1
            self.dma_tiles.append(t)
        return t

    def tls(self, name, n, **kw):
        return [self.tl("%s%d" % (name, i), **kw) for i in range(n)]

    def _wait(self, E, tk):
        if tk is None:
            return
        key, sem, val = tk
        if key == E.name and (E is self.PE or not SELF_SYNC):
            return
        if E.seen.get(key, 0) >= val:
            return
        E.e.wait_ge(sem, val)
        E.seen[key] = val

    def deps(self, E, reads, writes):
        for t in reads:
            self._wait(E, t.w)
            if t.psum:
                for key, (sem, val) in t.r.items():
                    if key != E.name:
                        self._wait(E, (key, sem, val))
        for t in writes:
            self._wait(E, t.w)
            for key, (sem, val) in t.r.items():
                self._wait(E, (key, sem, val))

    def _commit(self, tk, reads, writes):
        key, sem, val = tk
        for t in reads:
            if not t.const:
                t.r[key] = (sem, val)
        for t in writes:
            t.w = tk
            t.r = {}

    def op(self, E, reads, writes, fn):
        self.deps(E, reads, writes)
        ins = fn()
        E.cnt += 1
        ins.then_inc(E.sem, 1)
        self._commit((E.name, E.sem, E.cnt), reads, writes)

    def grp(self, E, reads, writes, fns):
        self.deps(E, reads, writes)
        ins = None
        for f in fns:
            ins = f()
        E.cnt += 1
        ins.then_inc(E.sem, 1)
        self._commit((E.name, E.sem, E.cnt), reads, writes)

    def dma(self, out, in_, reads, writes, dt, **kw):
        E = self.SP
        self.deps(E, reads, writes)
        ins = E.e.dma_start(out=out, in_=in_, **kw)
        dt.dcnt += 16
        ins.then_inc(dt.dsem, 16)
        self._commit((("d", dt.name), dt.dsem, dt.dcnt), reads, writes)

    def barrier(self):
        SP = self.SP
        for t in self.dma_tiles:
            if t.dcnt:
                self._wait(SP, (("d", t.name), t.dsem, t.dcnt))
        SP.e.sem_inc(SP.sem, 1)
        SP.cnt += 1
        engs = [self.PE, self.ACT, self.DVE, self.POOL, SP]
        snap = {E.name: E.cnt for E in engs}
        for E in engs:
            for E2 in engs:
                if E2 is E and E is SP:
                    continue
                v = snap[E2.name]
                if v and E.seen.get(E2.name, 0) < v:
                    E.e.wait_ge(E2.sem, v)
                    E.seen[E2.name] = v

    def final_wait(self, tiles):
        for t in tiles:
            self._wait(self.SP, t.w)


DEBUG_OUT = set()


def build(TA, TB, stage="full"):
    nc = bass.Bass("TRN2", target_bir_lowering=False)
    es = ExitStack()
    with es:
        _build(nc, es, TA, TB, stage)
    return nc


def _build(nc, es, TA, TB, stage):
    k = KB(nc, es)
    PE, ACT, DVE, POOL = k.PE, k.ACT, k.DVE, k.POOL
    pe, act, dve, pool = nc.tensor, nc.scalar, nc.vector, nc.gpsimd
    seqT = [TA, TB]

    def din(name, shape, dt=F32):
        return nc.dram_tensor(name, list(shape), dt, kind="ExternalInput").ap()

    def dscr(name, shape, dt=F32):
        if name in DEBUG_OUT:
            return nc.dram_tensor(name, list(shape), dt, kind="ExternalOutput").ap()
        return nc.dram_tensor(name, list(shape), dt).ap()

    x_in = [din("xa", [TA, D]), din("xb", [TB, D])]
    y_out = [nc.dram_tensor("ya", [TA, D], F32, kind="ExternalOutput").ap(),
             nc.dram_tensor("yb", [TB, D], F32, kind="ExternalOutput").ap()]
    cpack_d = din("cpack", [128, 128])
    pack1_d = din("pack1", [128, 128])
    pack2_d = din("pack2", [128, 128])
    ada_w_d = din("ada_w", [D, 9 * D])
    fw1_d = [din("f1_w1", [D, FF]), din("f2_w1", [D, FF])]
    fw3_d = [din("f1_w3", [D, FF]), din("f2_w3", [D, FF])]
    fw2_d = [din("f1_w2", [FF, D]), din("f2_w2", [FF, D])]
    w_in_d = din("w_in", [D, 2464])
    w_out_d = din("w_out", [D, D])
    pool_w_d = din("pool_w", [4, 128, 128])
    w2r_d = din("w2r", [128, 512])
    a2r_d = din("a2r", [128, 512])
    g2p_d = din("g2p", [256, 512])
    ident_d = din("ident", [128, 128])
    cmask_d = din("cmask", [2, 128, 640])
    bones_d = din("bones", [128, 128])
    pcorr_d = din("pcorr", [128, 2, 4, 8])

    w13s = [dscr("w13s%d" % f, [NJ, 128, 2048], BF16) for f in range(2)]
    w2s = [dscr("w2s%d" % f, [8, 128, FF], BF16) for f in range(2)]
    wins = dscr("wins", [NZ, 128, 1024], BF16)
    wouts = dscr("wouts", [8, 128, 1024], BF16)
    if "x1T0" in DEBUG_OUT:
        x1T = [dscr("x1T%d" % s, [D, seqT[s]]) for s in range(2)]
    else:
        x1T = [y_out[s].rearrange("t d -> (t d)").rearrange("(f t) -> f t", t=seqT[s]) for s in range(2)]
    zT = [dscr("zT%d" % s, [NZ * 128, seqT[s]], BF16) for s in range(2)]
    ybT = [dscr("ybT%d" % s, [512, seqT[s]]) for s in range(2)]
    bbT = [dscr("bbT%d" % s, [512, seqT[s]]) for s in range(2)]
    if "x2T0" in DEBUG_OUT:
        x2T = [dscr("x2T%d" % s, [D, seqT[s]]) for s in range(2)]
    else:
        x2T = None

    def x2_views(s, c0, c1):
        if x2T is not None:
            v = x2T[s].rearrange("(kc p) t -> p kc t", p=128)
            return [(v[:, 0:4, c0:c1], 0), (v[:, 4:8, c0:c1], 4)]
        return [(ybT[s].rearrange("(kc p) t -> p kc t", p=128)[:, :, c0:c1], 0), (bbT[s].rearrange("(kc p) t -> p kc t", p=128)[:, :, c0:c1], 4)]
    dr = {}

    def drt(name):
        if name not in dr:
            dr[name] = k.tl("dr_" + name)
        return dr[name]

    PFX = ["s_"]

    def sb(name, shape, dt, stack=es):
        return stack.enter_context(nc.sbuf_tensor(PFX[0] + name, list(shape), dt))

    ps = [es.enter_context(nc.psum_tensor("ps%d" % i, [128, 512], F32)) for i in range(7)]
    psb = es.enter_context(nc.psum_tensor("psb", [128, 1024], BF16))
    ps_t = k.tls("ps", 7)
    for t_ in ps_t:
        t_.psum = True
    psb_t = k.tl("psb")

    ident = sb("ident", [128, 128], F32)
    ident_t = k.tl("ident", const=True, dma=True)
    identb = sb("identb", [128, 4, 128], BF16)
    identb_t = k.tl("identb", const=True)
    onesm = sb("onesm", [128, 128], BF16)
    onesm_t = k.tl("onesm", const=True)
    bones = sb("bones", [128, 128], F32)
    bones_t = k.tl("bones", const=True, dma=True)
    bonesb = sb("bonesb", [128, 128], BF16)
    bavgb = sb("bavgb", [128, 128], BF16)
    bonesb_t = k.tl("bonesb", const=True)
    pk1T = sb("pk1T", [128, 128], F32)
    pk2T = sb("pk2T", [128, 128], F32)
    pk_t = k.tl("pk", const=True)
    sc = sb("sc", [128, 2, 9, 8], F32)
    sc_t = k.tl("sc", const=True)
    ommu = sb("ommu", [128, 16], F32)
    hmu = sb("hmu", [128, 16], F32)
    epsb = sb("epsb", [128, 4], F32)
    cns_t = k.tl("cns", const=True)

    k.dma(ident[:, :], ident_d[:, :], [], [ident_t], ident_t)
    k.dma(bones[:, :], bones_d[:, :], [], [bones_t], bones_t)
    k.op(DVE, [], [cns_t], lambda: dve.memset(epsb[:, 0:1], 1e-6))
    k.op(DVE, [], [cns_t], lambda: dve.memset(epsb[:, 1:2], 64e-5))
    k.op(DVE, [], [cns_t], lambda: dve.memset(epsb[:, 2:3], 1e-18))
    k.op(DVE, [], [cns_t], lambda: dve.memset(epsb[:, 3:4], 0.0))
    k.op(DVE, [], [onesm_t], lambda: dve.memset(onesm[:, :], 1.0 / D))
    for q in range(4):
        k.op(DVE, [ident_t], [identb_t], lambda q=q: dve.tensor_copy(out=identb[:, q, :], in_=ident[:, :]))
    k.op(DVE, [bones_t], [bonesb_t], lambda: dve.tensor_copy(out=bonesb[:, :], in_=bones[:, :]))
    k.op(DVE, [bones_t], [bonesb_t], lambda: dve.tensor_scalar(out=bavgb[:, :], in0=bones[:, :], scalar1=1.0 / 64, scalar2=None, op0=ALU.mult))

    dbg_done = set()

    def dbg(name, ap, shape, dt, tls_):
        if name not in DEBUG_OUT or name in dbg_done:
            return
        dbg_done.add(name)
        dd = nc.dram_tensor(name, list(shape), dt, kind="ExternalOutput").ap()
        dt_ = k.tl(name, dma=True)
        k.dma(dd, ap, tls_, [dt_], dt_)

    def mm(out, lhsT, rhs, start, stop):
        return lambda: pe.matmul(out, lhsT, rhs, start=start, stop=stop)

    with ExitStack() as p0:
        stg = sb("p0stg", [128, 3, 128], F32, p0)
        stg_t = k.tl("p0stg", dma=True)
        k.dma(stg[:, 0, :], cpack_d[:, :], [], [stg_t], stg_t)
        k.dma(stg[:, 1, :], pack1_d[:, :], [], [stg_t], stg_t)
        k.dma(stg[:, 2, :], pack2_d[:, :], [], [stg_t], stg_t)
        cT = sb("cT", [128, 128], F32, p0)
        scT = sb("scT", [128, 16], F32, p0)
        cT_t = k.tl("cT")
        k.grp(PE, [stg_t, ident_t], [ps_t[0]], [lambda q=q: pe.transpose(ps[0][:, q * 128:(q + 1) * 128], stg[:, q, :], ident[:, :]) for q in range(3)])
        k.op(ACT, [ps_t[0]], [cT_t], lambda: act.activation(out=scT[:, :], in_=ps[0][:, 0:16], func=AF.Silu))
        k.op(DVE, [ps_t[0]], [pk_t], lambda: dve.tensor_copy(out=pk1T[:, :], in_=ps[0][:, 128:256]))
        k.op(DVE, [ps_t[0]], [pk_t], lambda: dve.tensor_copy(out=pk2T[:, :], in_=ps[0][:, 256:384]))
        aw = [sb("aw%d" % i, [128, 8, 1024], F32, p0) for i in range(2)]
        aw_t = k.tls("aw", 2, dma=True)
        awv = ada_w_d.rearrange("(kc p) n -> p kc n", p=128)
        scv = scT[:, :].rearrange("p (s kc) -> p s kc", s=2)
        for m in range(9):
            b = m % 2
            k.dma(aw[b][:, :, :], awv[:, :, m * 1024:(m + 1) * 1024], [], [aw_t[b]], aw_t[b])
            for oc in range(8):
                col = (m * 8 + oc) * 2
                k.grp(PE, [aw_t[b], cT_t], [ps_t[1]],
                      [mm(ps[1][:, col:col + 2], aw[b][:, kc, oc * 128:(oc + 1) * 128], scv[:, :, kc], kc == 0, kc == 7) for kc in range(8)])
        mod = sb("mod", [128, 2, 72], F32, p0)
        mod_t = k.tl("mod")
        psv = ps[1][:, 0:144].rearrange("p (j s) -> p s j", s=2)
        for s in range(2):
            k.op(DVE, [ps_t[1], pk_t], [mod_t], lambda s=s: dve.tensor_tensor(out=mod[:, s, :], in0=psv[:, s, :], in1=pk1T[:, 0:72], op=ALU.add))
        tmp8 = sb("tmp8", [128, 8], F32, p0)
        t8_t = k.tl("tmp8")
        for s in range(2):
            for gi, (pre, post, half) in enumerate([(72, 80, 0.5), (88, 96, 1.0), (104, 112, 0.5)]):
                msh, msc, mg = 3 * gi, 3 * gi + 1, 3 * gi + 2
                k.op(DVE, [mod_t], [t8_t], lambda s=s, msc=msc: dve.tensor_scalar(out=tmp8[:, :], in0=mod[:, s, msc * 8:msc * 8 + 8], scalar1=1.0, scalar2=None, op0=ALU.add))
                k.op(DVE, [t8_t, pk_t], [sc_t], lambda s=s, gi=gi, pre=pre: dve.tensor_tensor(out=sc[:, s, 3 * gi, :], in0=tmp8[:, :], in1=pk1T[:, pre:pre + 8], op=ALU.mult))
                k.op(DVE, [mod_t], [sc_t], lambda s=s, gi=gi, msh=msh: dve.tensor_copy(out=sc[:, s, 3 * gi + 1, :], in_=mod[:, s, msh * 8:msh * 8 + 8]))
                k.op(DVE, [mod_t, pk_t], [sc_t], lambda s=s, gi=gi, mg=mg, post=post, half=half: dve.scalar_tensor_tensor(
                    out=sc[:, s, 3 * gi + 2, :], in0=mod[:, s, mg * 8:mg * 8 + 8], scalar=half, in1=pk1T[:, post:post + 8], op0=ALU.mult, op1=ALU.mult))
        k.op(DVE, [pk_t], [cns_t], lambda: dve.tensor_scalar(out=ommu[:, :], in0=pk2T[:, 40:56], scalar1=-1.0, scalar2=1.0, op0=ALU.mult, op1=ALU.add))
        k.op(DVE, [pk_t], [cns_t], lambda: dve.tensor_scalar(out=hmu[:, :], in0=pk2T[:, 40:56], scalar1=0.5, scalar2=None, op0=ALU.mult))
        k.barrier()
    if "dbg_sc" in DEBUG_OUT:
        dsc = nc.dram_tensor("dbg_sc", [128, 144], F32, kind="ExternalOutput").ap()
        dsc_t = k.tl("dbgsc", dma=True)
        k.dma(dsc[:, :], sc[:, :, :, :].rearrange("p a b c -> p (a b c)"), [sc_t], [dsc_t], dsc_t)
        dpk = nc.dram_tensor("dbg_pk", [128, 256], F32, kind="ExternalOutput").ap()
        k.dma(dpk[:, 0:128], pk1T[:, :], [pk_t], [dsc_t], dsc_t)
        k.dma(dpk[:, 128:256], pk2T[:, :], [pk_t], [dsc_t], dsc_t)
    W0 = lambda j: pk2T[:, j:j + 1]
    A0 = lambda j: pk2T[:, 8 + j:9 + j]
    KK, KA, RK, LG, LB, PSC = 16, 20, 24, 28, 32, 36

    with ExitStack() as pw:
        NWB = 4
        wst = [sb("wst%d" % i, [128, FF], F32, pw) for i in range(NWB)]
        wst_t = k.tls("wst", NWB, dma=True)
        wbf = [sb("wbf%d" % i, [128, FF], BF16, pw) for i in range(NWB)]
        wbf_t = k.tls("wbf", NWB, dma=True)
        jobs = []

        def conv(loads, ncols, dst, dst_t, zero=False):
            jobs.append((loads, ncols, dst, dst_t, zero))

        v3 = lambda t, o, n: t[:, o:o + n * 128].rearrange("p (a b) -> p a b", b=128)
        for f in range(2):
            w1v = fw1_d[f].rearrange("(kc p) n -> p kc n", p=128)
            w3v = fw3_d[f].rearrange("(kc p) n -> p kc n", p=128)
            w2v = fw2_d[f].rearrange("(jc p) n -> p jc n", p=128)
            for j in range(NJ):
                conv([(lambda t: v3(t, 0, 8), w1v[:, :, j * 128:(j + 1) * 128]),
                      (lambda t: v3(t, 1024, 8), w3v[:, :, j * 128:(j + 1) * 128])], 2048, w13s[f][j, :, :], drt("w13s%d" % f))
            for m in range(8):
                conv([(lambda t: v3(t, 0, NJ), w2v[:, :, m * 128:(m + 1) * 128])], FF, w2s[f][m, :, :], drt("w2s%d" % f))
        winv = w_in_d.rearrange("(kc p) n -> p kc n", p=128)
        for q in range(NZ):
            if q < NZ - 1:
                conv([(lambda t: v3(t, 0, 8), winv[:, :, q * 128:(q + 1) * 128])], 1024, wins[q, :, :], drt("wins"))
            else:
                conv([(lambda t: v3(t, 0, 8)[:, :, 0:32], winv[:, :, q * 128:q * 128 + 32])], 1024, wins[q, :, :], drt("wins"), zero=True)
        woutv = w_out_d.rearrange("(kc p) n -> p kc n", p=128)
        for m in range(8):
            conv([(lambda t: v3(t, 0, 8), woutv[:, :, m * 128:(m + 1) * 128])], 1024, wouts[m, :, :], drt("wouts"))
        engs3 = [(DVE, dve), (POOL, pool), (ACT, act)]

        def issue_loads(i):
            loads, ncols, dst, dst_t, zero = jobs[i]
            b = i % NWB
            if zero:
                k.op(DVE, [], [wst_t[b]], lambda: dve.memset(wst[b][:, 0:ncols], 0.0))
            for (o_ap, i_ap) in loads:
                k.dma(o_ap(wst[b]), i_ap, [], [wst_t[b]], wst_t[b])

        PRE = 2
        for i in range(min(PRE, len(jobs))):
            issue_loads(i)
        for i in range(len(jobs)):
            if i + PRE < len(jobs):
                issue_loads(i + PRE)
            loads, ncols, dst, dst_t, zero = jobs[i]
            b = i % NWB
            E, e = engs3[i % 3]
            if E is ACT:
                k.op(E, [wst_t[b]], [wbf_t[b]], lambda: act.activation(out=wbf[b][:, 0:ncols], in_=wst[b][:, 0:ncols], func=AF.Copy))
            else:
                k.op(E, [wst_t[b]], [wbf_t[b]], lambda: e.tensor_copy(out=wbf[b][:, 0:ncols], in_=wst[b][:, 0:ncols]))
            k.dma(dst, wbf[b][:, 0:ncols], [wbf_t[b]], [dst_t], wbf_t[b])
        k.barrier()

    def rsqrt_from_ps(pst, psa, epscol, out_ap, sd_ap, sd_t, out_t):
        k.op(ACT, [pst, cns_t], [sd_t], lambda: act.activation(out=sd_ap, in_=psa, func=AF.Ln, bias=epsb[:, epscol:epscol + 1], scale=1.0))
        k.op(ACT, [sd_t], [out_t], lambda: act.activation(out=out_ap, in_=sd_ap, func=AF.Exp, scale=-0.5))

    class FB:
        pass

    def alloc_ffn(st, with_xstage):
        B = FB()
        B.xfm = [sb("xfm%d" % i, [128, 8, TT], F32, st) for i in range(2)]
        B.xfm_t = [[k.tl("xfm%d_%d" % (i, j), dma=(j == 0)) for j in range(8)] for i in range(2)]
        if with_xstage:
            B.xst = [sb("xst0", [128, 4, D], F32, st)] * 2
            B.xst_t = [k.tl("xst", dma=True)] * 2
        B.sq = sb("sq", [128, 8, TT], BF16, st); B.sq_t = k.tl("sq"); B.sq2_t = k.tl("sq2")
        B.h = sb("h", [128, 8, TT], BF16, st); B.h_t = k.tls("h", 8)
        B.g = sb("g", [128, NJ, TT], BF16, st); B.g_t = k.tls("g", NJ)
        B.osb = sb("osb", [128, 8, TT], F32, st); B.osb_t = k.tls("osb", 8)
        B.tmp = [sb("ftmp%d" % i, [128, TT], F32, st) for i in range(2)]; B.tmp_t = k.tls("ftmp", 2)
        B.sa = [sb("fsa%d" % i, [128, TT], F32, st) for i in range(2)]; B.sa_t = k.tls("fsa", 2)
        B.sd = sb("fsd", [128, TT], F32, st); B.sd_t = k.tl("fsd")
        B.rstd = sb("rstd", [128, TT], F32, st); B.rstd_t = k.tl("rstd")
        B.w13 = [sb("w13b%d" % i, [128, 2, 8, 128], BF16, st) for i in range(4)]; B.w13_t = k.tls("w13b", 4, dma=True)
        B.w2 = [sb("w2b%d" % i, [128, NJ, 128], BF16, st) for i in range(4)]; B.w2_t = k.tls("w2b", 4, dma=True)
        B.n13 = 0
        B.n2 = 0
        B.q13pre = None
        B.q2 = []
        B.W = TT
        return B

    def pre_norm(B, xfm, xfm_t, s, kind):
        k.op(POOL, xfm_t[0:4], [B.sq_t], lambda: pool.tensor_tensor(out=B.sq[:, 0:4, :], in0=xfm[:, 0:4, :], in1=xfm[:, 0:4, :], op=ALU.mult))
        k.op(DVE, xfm_t[4:8], [B.sq2_t], lambda: dve.tensor_tensor(out=B.sq[:, 4:8, :], in0=xfm[:, 4:8, :], in1=xfm[:, 4:8, :], op=ALU.mult))
        k.grp(PE, [B.sq_t, B.sq2_t, onesm_t], [ps_t[6]], [mm(ps[6][:, 0:B.W], onesm[:, :], B.sq[:, kc, :], kc == 0, kc == 7) for kc in range(8)])
        rsqrt_from_ps(ps_t[6], ps[6][:, 0:B.W], 0, B.rstd[:, :], B.sd[:, :], B.sd_t, B.rstd_t)
        for kc in range(8):
            tb = kc % 2
            k.op(DVE, [xfm_t[kc], B.rstd_t, sc_t], [B.tmp_t[tb]], lambda kc=kc, tb=tb: dve.scalar_tensor_tensor(
                out=B.tmp[tb][:, :], in0=xfm[:, kc, :], scalar=sc[:, s, kind, kc:kc + 1], in1=B.rstd[:, :], op0=ALU.mult, op1=ALU.mult))
            k.op(ACT, [B.tmp_t[tb], sc_t], [B.h_t[kc]], lambda kc=kc, tb=tb: act.activation(
                out=B.h[:, kc, :], in_=B.tmp[tb][:, :], func=AF.Identity, bias=sc[:, s, kind + 1, kc:kc + 1], scale=1.0))

    def post_norm_residual(B, xfm, xfm_t, s, kind):
        k.op(POOL, B.osb_t[0:4], [B.sq_t], lambda: pool.tensor_tensor(out=B.sq[:, 0:4, :], in0=B.osb[:, 0:4, :], in1=B.osb[:, 0:4, :], op=ALU.mult))
        k.op(DVE, B.osb_t[4:8], [B.sq2_t], lambda: dve.tensor_tensor(out=B.sq[:, 4:8, :], in0=B.osb[:, 4:8, :], in1=B.osb[:, 4:8, :], op=ALU.mult))
        k.grp(PE, [B.sq_t, B.sq2_t, onesm_t], [ps_t[6]], [mm(ps[6][:, 0:B.W], onesm[:, :], B.sq[:, kc, :], kc == 0, kc == 7) for kc in range(8)])
        rsqrt_from_ps(ps_t[6], ps[6][:, 0:B.W], 0, B.rstd[:, :], B.sd[:, :], B.sd_t, B.rstd_t)
        for kc in range(8):
            tb = kc % 2
            k.op(DVE, [B.osb_t[kc], B.rstd_t, sc_t], [B.tmp_t[tb]], lambda kc=kc, tb=tb: dve.scalar_tensor_tensor(
                out=B.tmp[tb][:, :], in0=B.osb[:, kc, :], scalar=sc[:, s, kind + 2, kc:kc + 1], in1=B.rstd[:, :], op0=ALU.mult, op1=ALU.mult))
            k.op(POOL, [B.tmp_t[tb], xfm_t[kc]], [xfm_t[kc]], lambda kc=kc, tb=tb: pool.tensor_tensor(
                out=xfm[:, kc, :], in0=xfm[:, kc, :], in1=B.tmp[tb][:, :], op=ALU.add))

    def _ld13(B, f, j):
        b = B.n13 % 4
        B.n13 += 1
        k.dma(B.w13[b][:, :, :, :].rearrange("p a b c -> p (a b c)"), w13s[f][j, :, :], [drt("w13s%d" % f)], [B.w13_t[b]], B.w13_t[b])
        return b

    def _ld2(B, f, m):
        b = B.n2 % 4
        B.n2 += 1
        k.dma(B.w2[b][:, :, :].rearrange("p a b -> p (a b)"), w2s[f][m, :, :], [drt("w2s%d" % f)], [B.w2_t[b]], B.w2_t[b])
        return b

    def ffn_ab(B, f):
        q13 = B.q13pre if B.q13pre else [_ld13(B, f, 0), _ld13(B, f, 1), _ld13(B, f, 2)]
        B.q13pre = None
        B.q2 = []
        for j in range(NJ):
            if j + 3 < NJ:
                q13.append(_ld13(B, f, j + 3))
            if j >= NJ - 3:
                B.q2.append(_ld2(B, f, j - (NJ - 3)))
            wb = q13[j]
            pa, pb = (j % 2) * 2, (j % 2) * 2 + 1
            k.grp(PE, B.h_t + [B.w13_t[wb]], [ps_t[pa]], [mm(ps[pa][:, :], B.w13[wb][:, 0, kc, :], B.h[:, kc, :], kc == 0, kc == 7) for kc in range(8)])
            k.grp(PE, B.h_t + [B.w13_t[wb]], [ps_t[pb]], [mm(ps[pb][:, :], B.w13[wb][:, 1, kc, :], B.h[:, kc, :], kc == 0, kc == 7) for kc in range(8)])
            sbi = j % 2
            k.op(ACT, [ps_t[pa]], [B.sa_t[sbi]], lambda pa=pa, sbi=sbi: act.activation(out=B.sa[sbi][:, :], in_=ps[pa][:, :], func=AF.Silu))
            k.op(DVE, [B.sa_t[sbi], ps_t[pb]], [B.g_t[j]], lambda pb=pb, sbi=sbi, j=j: dve.tensor_tensor(out=B.g[:, j, :], in0=B.sa[sbi][:, :], in1=ps[pb][:, :], op=ALU.mult))

    def ffn_o(B, f, prefetch_next):
        q2 = B.q2
        for m in range(8):
            if m + 3 < 8:
                q2.append(_ld2(B, f, m + 3))
            if m == 6 and prefetch_next:
                B.q13pre = [_ld13(B, f, 0), _ld13(B, f, 1), _ld13(B, f, 2)]
            wb = q2[m]
            po = 4 + (m % 2)
            k.grp(PE, B.g_t + [B.w2_t[wb]], [ps_t[po]], [mm(ps[po][:, :], B.w2[wb][:, jc, :], B.g[:, jc, :], jc == 0, jc == NJ - 1) for jc in range(NJ)])
            k.op(ACT, [ps_t[po]], [B.osb_t[m]], lambda po=po, m=m: act.activation(out=B.osb[:, m, :], in_=ps[po][:, :], func=AF.Copy))

    def load_x_tokmajor(B, s, t0, xb):
        sbi = xb
        k.dma(B.xst[sbi][:, :, :], x_in[s][t0:t0 + TT, :].rearrange("(b p) d -> p b d", p=128), [], [B.xst_t[sbi]], B.xst_t[sbi])

    def transpose_in(B, xb):
        for kc in range(8):
            pb = 4 + (kc % 2)
            k.grp(PE, [B.xst_t[xb], ident_t], [ps_t[pb]], [lambda kc=kc, pb=pb, q=q: pe.transpose(ps[pb][:, q * 128:(q + 1) * 128], B.xst[xb][:, q, kc * 128:(kc + 1) * 128], ident[:, :]) for q in range(4)])
            if kc % 2 == 0:
                k.op(ACT, [ps_t[pb]], [B.xfm_t[xb][kc]], lambda kc=kc, pb=pb: act.activation(out=B.xfm[xb][:, kc, :], in_=ps[pb][:, :], func=AF.Copy))
            else:
                k.op(DVE, [ps_t[pb]], [B.xfm_t[xb][kc]], lambda kc=kc, pb=pb: dve.tensor_copy(out=B.xfm[xb][:, kc, :], in_=ps[pb][:, :]))

    tiles = [(s, t0) for s in range(2) for t0 in range(0, seqT[s], TT)]

    with ExitStack() as pa_:
        PFX[0] = "A_"
        B = alloc_ffn(pa_, True)
        win = sb("winb", [128, NZ, 8, 128], BF16, pa_)
        win_t = k.tl("winb", const=True, dma=True)
        for q in range(NZ):
            k.dma(win[:, q, :, :].rearrange("p a b -> p (a b)"), wins[q, :, :], [drt("wins")], [win_t], win_t)
        zst = [sb("zst%d" % i, [128, TT], BF16, pa_) for i in range(3)]
        zst_t = k.tls("zst", 3, dma=True)
        print("pass A sbuf remaining", nc.sbuf_bytes_remaining, flush=True)
        h1, h1_t = B.h, B.h_t
        h2 = sb("h2", [128, 8, TT], BF16, pa_)
        h2_t = k.tls("h2_", 8)
        load_x_tokmajor(B, tiles[0][0], tiles[0][1], 0)
        transpose_in(B, 0)
        if len(tiles) > 1:
            load_x_tokmajor(B, tiles[1][0], tiles[1][1], 1)
        pre_norm(B, B.xfm[0], B.xfm_t[0], tiles[0][0], 0)
        def z_phase(s, t0):
            for q in range(NZ):
                pz = q % 4
                zb = q % 3
                k.grp(PE, h2_t + [win_t], [ps_t[pz]], [mm(ps[pz][:, :], win[:, q, kc, :], h2[:, kc, :], kc == 0, kc == 7) for kc in range(8)])
                if q % 2 == 0:
                    k.op(ACT, [ps_t[pz]], [zst_t[zb]], lambda pz=pz, zb=zb: act.activation(out=zst[zb][:, :], in_=ps[pz][:, :], func=AF.Copy))
                else:
                    k.op(DVE, [ps_t[pz]], [zst_t[zb]], lambda pz=pz, zb=zb: dve.tensor_copy(out=zst[zb][:, :], in_=ps[pz][:, :]))
                k.dma(zT[s][q * 128:(q + 1) * 128, t0:t0 + TT], zst[zb][:, :], [zst_t[zb]], [drt("zT%d" % s)], zst_t[zb])

        for ti, (s, t0) in enumerate(tiles):
            xb = ti % 2
            xfm, xfm_t = B.xfm[xb], B.xfm_t[xb]
            last = ti + 1 == len(tiles)
            ffn_ab(B, 0)
            if ti > 0:
                z_phase(*tiles[ti - 1])
            if not last:
                transpose_in(B, 1 - xb)
                if ti + 2 < len(tiles):
                    load_x_tokmajor(B, tiles[ti + 2][0], tiles[ti + 2][1], xb)
                pre_norm(B, B.xfm[1 - xb], B.xfm_t[1 - xb], tiles[ti + 1][0], 0)
            ffn_o(B, 0, not last)
            post_norm_residual(B, xfm, xfm_t, s, 0)
            k.dma(x1T[s].rearrange("(kc p) t -> p kc t", p=128)[:, :, t0:t0 + TT], xfm[:, :, :], xfm_t, [drt("x1T%d" % s)], xfm_t[0])
            B.h, B.h_t = h2, h2_t
            pre_norm(B, xfm, xfm_t, s, 3)
            B.h, B.h_t = h1, h1_t
        z_phase(*tiles[-1])
        k.barrier()
    if stage == "A":
        return

    ST = 256
    NCK = ST // CH
    ZR = 16

    def scan_pass(d, with_mix, stage_stop=None):
        with ExitStack() as st:
            PFX[0] = "C_" if with_mix else "B_"
            cm = sb("cm", [128, 640], F32, st)
            cmT4 = sb("cmT4", [128, 4, 128], F32, st)
            w2b = sb("w2b", [128, 512], BF16, st)
            a2b = sb("a2b", [128, 512], BF16, st)
            ones128 = sb("ones128", [128, 128], F32, st)
            cst = sb("cst", [128, 512], F32, st)
            cst_t = k.tl("cst", dma=True)
            pc_t = k.tl("pconst", const=True, dma=True)
            k.dma(cm[:, :], cmask_d[d, :, :], [], [pc_t], pc_t)
            for q in range(4):
                k.dma(cmT4[:, q, :], cmask_d[d, :, 512:640], [], [pc_t], pc_t)
            k.op(DVE, [], [pc_t], lambda: dve.memset(ones128[:, :], 1.0))
            k.dma(cst[:, :], w2r_d[:, :], [], [cst_t], cst_t)
            k.op(DVE, [cst_t], [pc_t], lambda: dve.tensor_copy(out=w2b[:, :], in_=cst[:, :]))
            k.dma(cst[:, :], a2r_d[:, :], [cst_t], [cst_t], cst_t)
            k.op(DVE, [cst_t], [pc_t], lambda: dve.tensor_copy(out=a2b[:, :], in_=cst[:, :]))
            if with_mix:
                g2b = sb("g2b", [128, 2, 512], BF16, st)
                pwb = sb("pwb", [128, 4, 128], BF16, st)
                woutb = sb("woutb", [128, 8, 8, 128], BF16, st)
                pcr = sb("pcr", [128, 2, 4, 8], F32, st)
                for q in range(2):
                    k.dma(cst[:, :], g2p_d[q * 128:(q + 1) * 128, :], [cst_t], [cst_t], cst_t)
                    k.op(DVE, [cst_t], [pc_t], lambda q=q: dve.tensor_copy(out=g2b[:, q, :], in_=cst[:, :]))
                k.dma(cst[:, :].rearrange("p (a b) -> p a b", b=128), pool_w_d.rearrange("g c d -> c g d"), [cst_t], [cst_t], cst_t)
                k.op(DVE, [cst_t], [pc_t], lambda: dve.tensor_copy(out=pwb[:, :, :].rearrange("p a b -> p (a b)"), in_=cst[:, :]))
                for m in range(8):
                    k.dma(woutb[:, m, :, :].rearrange("p a b -> p (a b)"), wouts[m, :, :], [drt("wouts")], [pc_t], pc_t)
                k.dma(pcr[:, :, :, :], pcorr_d[:, :, :, :], [], [pc_t], pc_t)
            k.barrier()
            zl = sb("zl", [128, ZR, ST + 2], BF16, st); zl_t = k.tl("zl", dma=True)
            zs = sb("zs", [128, ZR, ST], F32, st); zs_t = k.tls("zs", ZR)
            tA = [sb("stA%d" % i, [128, ST], F32, st) for i in range(2)]; tA_t = k.tls("stA", 2)
            tB = [sb("stB%d" % i, [128, ST], F32, st) for i in range(2)]; tB_t = k.tls("stB", 2)
            twb = sb("twb", [128, ST], BF16, st); twb_t = k.tl("twb")
            zab = sb("zab", [128, ST], BF16, st); zab_t = k.tl("zab")
            sg = sb("sg", [128, 4, ST], F32, st); sg_t = k.tl("sg", dma=with_mix)
            aa = sb("aa", [128, 4, ST], F32, st); aa_t = k.tl("aa", dma=with_mix)
            cumee = sb("cumee", [128, 2, 4, ST], F32, st)
            cum = cumee[:, 0, :, :]; ee = cumee[:, 1, :, :]
            cum_t = k.tl("cumee", dma=with_mix); ee_t = cum_t
            tot = sb("tot", [128, 3, 4 * NCK], F32, st); tot_t = k.tl("tot")
            Wbuf = sb("Wbuf", [128, 3, 4, ST], F32, st)
            winc = Wbuf[:, 0, :, :]; winv = Wbuf[:, 1, :, :]; wexc = Wbuf[:, 2, :, :]
            wx_t = k.tl("wx")
            kkr_f = sb("kkr", [128, 4, ST + 16], F32, st); kkr = kkr_f[:, :, 0:ST]; kkr_t = k.tl("kkr")
            sqk = sb("sqk", [128, 4, ST], BF16, st); sqk_t = k.tl("sqk")
            sdk = sb("sdk", [128, 4, ST], F32, st); sdk_t = k.tl("sdk")
            kk_f = sb("kk", [128, 4, ST + 16], F32, st); kk = kk_f[:, :, 0:ST]; kk_t = k.tl("kk")
            kd_f = sb("kd", [128, 4, ST + 16], F32, st); kd = kd_f[:, :, 0:ST]; kd_t = k.tl("kd")
            t1 = sb("t1", [128, 4, ST], F32, st); t1_t = k.tl("t1")
            bb_f = sb("bbq", [128, 4, ST + 16], F32, st); bb = bb_f[:, :, 0:ST]; bb_t = k.tl("bbq", dma=with_mix)
            rkb = sb("rkb", [128, 4, ST], BF16, st); rkb_t = k.tl("rkb")
            bon = sb("bon", [128, 4, ST], F32, st); bon_t = k.tl("bon", dma=True)
            KR = sb("KR", [128, 4, NCK, 2, CH], BF16, st); KR_t = k.tl("KR")
            ktbt = sb("ktbt", [128, 2, 4, ST], BF16, st)
            kt = ktbt[:, 0, :, :]; bt = ktbt[:, 1, :, :]
            kt_t = k.tl("ktbt"); bt_t = kt_t
            vb = sb("vb", [128, 4, ST], BF16, st); vb_t = k.tl("vb")
            ysb = sb("ysb", [128, 4, ST], F32, st); ysb_t = k.tl("ysb", dma=True)
            A4c = [sb("A4_%d" % c, [128, 8, 512], BF16, st) for c in range(NCK)]; A4c_t = [k.tls("A4_%d_" % c, 8) for c in range(NCK)]
            Nbc = [[sb("Nb%d_%d" % (c, i), [128, 8, CH], BF16, st) for i in range(2)] for c in range(NCK)]
            Nbc_t = [[k.tls("Nb%d_%d_" % (c, i), 2) for i in range(2)] for c in range(NCK)]
            NTbc = [[sb("NTb%d_%d" % (c, i), [128, 8, CH], BF16, st) for i in range(2)] for c in range(NCK)]
            NTbc_t = [[k.tls("NTb%d_%d_" % (c, i), 2) for i in range(2)] for c in range(NCK)]
            Zbc = [[sb("Zb%d_%d" % (c, i), [128, 8, CH], BF16, st) for i in range(2)] for c in range(NCK)]
            Zbc_t = [[k.tls("Zb%d_%d_" % (c, i), 2) for i in range(2)] for c in range(NCK)]
            VtZc = [sb("VtZ%d" % c, [128, 8, CH], BF16, st) for c in range(NCK)]; VtZc_t = k.tls("VtZ", NCK)
            UnZ = sb("UnZ", [128, 8, CH], BF16, st); UnZ_t = k.tl("UnZ")
            Ktokc = [sb("Ktok%d" % c, [128, 4, CH], BF16, st) for c in range(NCK)]
            Btokc = [sb("Btok%d" % c, [128, 4, CH], BF16, st) for c in range(NCK)]; KBc_t = k.tls("KBtok", NCK)
            X1sb = sb("X1sb", [128, 4, CH], BF16, st); X1_t = k.tl("X1sb")
            Gbd = [sb("Gbd%d" % i, [128, 4, CH], BF16, st) for i in range(2)]; Gbd_t = k.tls("Gbd", 2)
            psb6 = ps[6][:, :].bitcast(BF16)
            psb5 = ps[5][:, :].bitcast(BF16)
            for c in range(NCK):
                k.op(POOL, [], [VtZc_t[c]], lambda c=c: pool.memset(VtZc[c][:, :, :], 0.0))
            k.op(POOL, [], [UnZ_t], lambda: pool.memset(UnZ[:, :, :], 0.0))
            if with_mix:
                ybl = sg; ybl_t = sg_t
                bbl = aa; bbl_t = aa_t
                xfm = cumee[:, :, :, :].rearrange("p a c t -> p (a c) t"); xfm_t = [cum_t] * 8
                zp = sb("zp", [128, 4, ST + 16], BF16, st); zp_t = k.tl("zp", dma=True)
                s2 = kkr_f; s4 = kk_f; s8 = kd_f
                s16 = sb("s16", [128, 1, ST + 16], F32, st)
                sp_t = k.tl("spool")
                spl = [kkr_t, kk_t, kd_t, sp_t]
                pp = rkb; pp_t = rkb_t
                mix = KR[:, :, :, :, :].rearrange("p c i a t -> p (c i a t)").rearrange("p (k t) -> p k t", t=ST); mix_t = [KR_t] * 8
                ybf = sqk; ybf_t = sqk_t
                yc = t1; yc_t = t1_t
                sgz = sb("sgz", [128, 2, ST], BF16, st); sgz_t = k.tl("sgz")
                Bc = FB()
                Bc.sq = ktbt[:, :, :, :].rearrange("p a c t -> p (a c) t"); Bc.sq_t = kt_t; Bc.sq2_t = kt_t
                Bc.osb = Wbuf[:, 0:2, :, :].rearrange("p a c t -> p (a c) t"); Bc.osb_t = [wx_t] * 8
                Bc.tmp = [sb("ctmp%d" % i, [128, ST], F32, st) for i in range(2)]; Bc.tmp_t = k.tls("ctmp", 2)
                Bc.sd = sb("csd", [128, ST], F32, st); Bc.sd_t = k.tl("csd")
                Bc.rstd = sb("crstd", [128, ST], F32, st); Bc.rstd_t = k.tl("crstd")
                Bc.W = ST

                def mix_tail(s, t0, T, zrows):
                    yrows = lambda tns: tns.rearrange("(c p) t -> p c t", p=128)[:, :, t0:t0 + ST]
                    k.dma(ybl[:, :, :], yrows(ybT[s]), [drt("ybT%d" % s)], [ybl_t], ybl_t)
                    k.dma(bbl[:, :, :], yrows(bbT[s]), [drt("bbT%d" % s)], [bbl_t], bbl_t)
                    k.dma(xfm[:, :, :], x1T[s].rearrange("(kc p) t -> p kc t", p=128)[:, :, t0:t0 + ST], [drt("x1T%d" % s)], xfm_t, xfm_t[0])
                    lo, hi = max(t0 - 8, 0), min(t0 + ST + 8, T)
                    off = lo - (t0 - 8)
                    if t0 == 0:
                        k.op(POOL, [], [zp_t], lambda: pool.memset(zp[:, :, 0:8], 0.0))
                    if t0 + ST == T:
                        k.op(POOL, [], [zp_t], lambda: pool.memset(zp[:, :, ST + 8:ST + 16], 0.0))
                    k.dma(zp[:, :, off:off + hi - lo], zrows[:, 0:4, lo:hi], [drt("zT%d" % s)], [zp_t], zp_t)
                    dbg("dbg_yf", ysb[:, :, :], [128, 4, ST], F32, [ysb_t])
                    dbg("dbg_ybl", ybl[:, :, :], [128, 4, ST], F32, [ybl_t])
                    k.op(POOL, [ysb_t, ybl_t], [ybl_t], lambda: pool.tensor_tensor(out=ybl[:, :, :], in0=ysb[:, :, :], in1=ybl[:, :, :], op=ALU.add))
                    k.op(POOL, [bon_t, bbl_t], [bbl_t], lambda: pool.tensor_tensor(out=bbl[:, :, :], in0=bon[:, :, :], in1=bbl[:, :, :], op=ALU.add))
                    k.op(ACT, [ybl_t], [ybf_t], lambda: act.activation(out=ybf[:, :, :], in_=ybl[:, :, :], func=AF.Copy))
                    for half in range(2):
                        pbk = half
                        k.grp(PE, [ybf_t, bonesb_t], [ps_t[pbk]], [mm(ps[pbk][:, c2 * ST:(c2 + 1) * ST], bavgb[:, :], ybf[:, half * 2 + c2, :], True, True) for c2 in range(2)])
                        k.op(DVE, [ps_t[pbk], ybl_t], [yc_t], lambda half=half, pbk=pbk: dve.tensor_tensor(
                            out=yc[:, half * 2:half * 2 + 2, :], in0=ybl[:, half * 2:half * 2 + 2, :], in1=ps[pbk][:, :].rearrange("p (a b) -> p a b", b=ST), op=ALU.subtract))
                    k.op(POOL, [yc_t], [ybf_t], lambda: pool.tensor_tensor(out=ybf[:, :, :], in0=yc[:, :, :], in1=yc[:, :, :], op=ALU.mult))
                    for half in range(2):
                        pbk = 2 + half
                        k.grp(PE, [ybf_t, bonesb_t], [ps_t[pbk]], [mm(ps[pbk][:, c2 * ST:(c2 + 1) * ST], bavgb[:, :], ybf[:, half * 2 + c2, :], True, True) for c2 in range(2)])
                        k.op(ACT, [ps_t[pbk], cns_t], [sdk_t], lambda half=half, pbk=pbk: act.activation(out=sdk[:, half * 2:half * 2 + 2, :].rearrange("p a b -> p (a b)"), in_=ps[pbk][:, :], func=AF.Ln, bias=epsb[:, 1:2], scale=1.0))
                    k.op(ACT, [sdk_t], [sdk_t], lambda: act.activation(out=sdk[:, :, :], in_=sdk[:, :, :], func=AF.Exp, scale=-0.5))
                    k.op(POOL, [yc_t, sdk_t], [yc_t], lambda: pool.tensor_tensor(out=yc[:, :, :], in0=yc[:, :, :], in1=sdk[:, :, :], op=ALU.mult))
                    for cc in range(4):
                        k.op(DVE, [yc_t, pk_t], [yc_t], lambda cc=cc: dve.tensor_scalar(out=yc[:, cc, :], in0=yc[:, cc, :], scalar1=pk2T[:, LG + cc:LG + cc + 1], scalar2=pk2T[:, LB + cc:LB + cc + 1], op0=ALU.mult, op1=ALU.add))
                    k.op(POOL, [yc_t, bbl_t], [yc_t], lambda: pool.tensor_tensor(out=yc[:, :, :], in0=yc[:, :, :], in1=bbl[:, :, :], op=ALU.add))
                    for q in range(2):
                        k.op(ACT, [zs_t[14 + q]], [sgz_t], lambda q=q: act.activation(out=sgz[:, q, :], in_=zs[:, 14 + q, :], func=AF.Sigmoid))
                    for half in range(2):
                        pbk = 4 + half if half == 0 else 0
                        fns = []
                        for c2 in range(2):
                            cc = half * 2 + c2
                            fns.append(mm(ps[pbk][:, c2 * ST:(c2 + 1) * ST], g2b[:, 0, cc * 128:(cc + 1) * 128], sgz[:, 0, :], True, False))
                            fns.append(mm(ps[pbk][:, c2 * ST:(c2 + 1) * ST], g2b[:, 1, cc * 128:(cc + 1) * 128], sgz[:, 1, :], False, True))
                        k.grp(PE, [sgz_t, pc_t], [ps_t[pbk]], fns)
                        k.op(DVE, [ps_t[pbk], yc_t], mix_t[4 + half * 2:6 + half * 2], lambda half=half, pbk=pbk: dve.tensor_tensor(
                            out=mix[:, 4 + half * 2:6 + half * 2, :], in0=yc[:, half * 2:half * 2 + 2, :], in1=ps[pbk][:, :].rearrange("p (a b) -> p a b", b=ST), op=ALU.mult))
                    L_ = ST + 16
                    k.op(POOL, [zp_t], spl, lambda: pool.tensor_tensor(out=s2[:, :, 0:L_ - 1], in0=zp[:, :, 0:L_ - 1], in1=zp[:, :, 1:L_], op=ALU.add))
                    k.op(POOL, spl, spl, lambda: pool.tensor_tensor(out=s4[:, 1:4, 0:L_ - 3], in0=s2[:, 1:4, 0:L_ - 3], in1=s2[:, 1:4, 2:L_ - 1], op=ALU.add))
                    k.op(POOL, spl, spl, lambda: pool.tensor_tensor(out=s8[:, 2:4, 0:L_ - 7], in0=s4[:, 2:4, 0:L_ - 7], in1=s4[:, 2:4, 4:L_ - 3], op=ALU.add))
                    k.op(POOL, spl, spl, lambda: pool.tensor_tensor(out=s16[:, 0, 0:L_ - 15], in0=s8[:, 3, 0:L_ - 15], in1=s8[:, 3, 8:L_ - 7], op=ALU.add))
                    wsum = [s2[:, 0, 7:7 + ST], s4[:, 1, 6:6 + ST], s8[:, 2, 4:4 + ST], s16[:, 0, 0:ST]]
                    for g in range(4):
                        if t0 == 0:
                            k.op(DVE, spl + [pc_t], spl, lambda g=g: dve.tensor_tensor(out=wsum[g][:, 0:8], in0=wsum[g][:, 0:8], in1=pcr[:, 0, g, :], op=ALU.mult))
                        if t0 + ST == T:
                            k.op(DVE, spl + [pc_t], spl, lambda g=g: dve.tensor_tensor(out=wsum[g][:, ST - 8:ST], in0=wsum[g][:, ST - 8:ST], in1=pcr[:, 1, g, :], op=ALU.mult))
                        k.op(DVE, spl + [zp_t], [pp_t], lambda g=g: dve.scalar_tensor_tensor(out=pp[:, g, :], in0=wsum[g], scalar=1.0 / (2 << g), in1=zp[:, g, 8:8 + ST], op0=ALU.mult, op1=ALU.subtract))
                    for half in range(2):
                        pbk = 1 + half
                        k.grp(PE, [pp_t, pc_t], [ps_t[pbk]], [mm(ps[pbk][:, c2 * ST:(c2 + 1) * ST], pwb[:, half * 2 + c2, :], pp[:, half * 2 + c2, :], True, True) for c2 in range(2)])
                        for c2 in range(2):
                            g = half * 2 + c2
                            k.op(ACT, [ps_t[pbk], pk_t], [mix_t[g]], lambda g=g, c2=c2, pbk=pbk: act.activation(out=mix[:, g, :], in_=ps[pbk][:, c2 * ST:(c2 + 1) * ST], func=AF.Copy, scale=pk2T[:, PSC + g:PSC + g + 1]))
                    dbg("dbg_mix", mix[:, :, :], [128, 8, ST], BF16, mix_t)
                    dbg("dbg_yc", yc[:, :, :], [128, 4, ST], F32, [yc_t])
                    for mo in range(8):
                        pbk = 3 + (mo % 2)
                        k.grp(PE, mix_t + [pc_t], [ps_t[pbk]], [mm(ps[pbk][:, 0:ST], woutb[:, mo, kc, :], mix[:, kc, :], kc == 0, kc == 7) for kc in range(8)])
                        k.op(ACT, [ps_t[pbk]], [Bc.osb_t[mo]], lambda mo=mo, pbk=pbk: act.activation(out=Bc.osb[:, mo, :], in_=ps[pbk][:, 0:ST], func=AF.Copy))
                    post_norm_residual(Bc, xfm, xfm_t, s, 3)
                    for (dv, ko) in x2_views(s, t0, t0 + ST):
                        k.dma(dv, xfm[:, ko:ko + 4, :], xfm_t + [ybl_t, bbl_t], [drt("x2T%d" % s)], xfm_t[0])

            print("scan pass sbuf remaining", nc.sbuf_bytes_remaining, flush=True)
            def load_zl(s, sti):
                T = seqT[s]
                t0 = sti * ST
                zrows = zT[s].rearrange("(j p) t -> p j t", p=128)
                lo, hi = max(t0 - 1, 0), min(t0 + ST + 1, T)
                off = lo - (t0 - 1)
                if t0 == 0:
                    k.op(POOL, [], [zl_t], lambda: pool.memset(zl[:, :, 0:1], 0.0))
                if t0 + ST == T:
                    k.op(POOL, [], [zl_t], lambda: pool.memset(zl[:, :, ST + 1:ST + 2], 0.0))
                k.dma(zl[:, :, off:off + hi - lo], zrows[:, 4:4 + ZR, lo:hi], [drt("zT%d" % s)], [zl_t], zl_t)

            work = [(s, sti) for s in range(2) for sti in (range(seqT[s] // ST) if d == 0 else range(seqT[s] // ST - 1, -1, -1))]
            load_zl(*work[0])
            gcur = 0
            for wi, (s, sti) in enumerate(work):
                T = seqT[s]
                zrows = zT[s].rearrange("(j p) t -> p j t", p=128)
                if wi == 0 or work[wi - 1][0] != s:
                    gcur = 0
                    k.op(POOL, [], [Gbd_t[0]], lambda: pool.memset(Gbd[0][:, :, :], 0.0))
                if True:
                    t0 = sti * ST
                    rows = list(range(14)) + ([14, 15] if with_mix else [])
                    for ri, j in enumerate(rows):
                        b2 = ri % 2
                        k.op(POOL, [zl_t], [tA_t[b2]], lambda j=j, b2=b2: pool.tensor_tensor(out=tA[b2][:, :], in0=zl[:, j, 0:ST], in1=zl[:, j, 2:ST + 2], op=ALU.add))
                        k.op(ACT, [zl_t, cns_t], [tB_t[b2]], lambda j=j, b2=b2: act.activation(out=tB[b2][:, :], in_=zl[:, j, 1:ST + 1], func=AF.Copy, scale=ommu[:, j:j + 1]))
                        k.op(DVE, [tA_t[b2], tB_t[b2], cns_t], [zs_t[j]], lambda j=j, b2=b2: dve.scalar_tensor_tensor(
                            out=zs[:, j, :], in0=tA[b2][:, :], scalar=hmu[:, j:j + 1], in1=tB[b2][:, :], op0=ALU.mult, op1=ALU.add))
                    if wi + 1 < len(work):
                        load_zl(*work[wi + 1])
                    R_, K_, V_ = 0, 4, 8
                    v4 = lambda t_: t_[:, :, :].rearrange("p c (i t) -> p c i t", t=CH)
                    k.op(ACT, [zs_t[12]], [twb_t], lambda: act.activation(out=twb[:, :], in_=zs[:, 12, :], func=AF.Tanh))
                    k.op(POOL, [zs_t[13]], [zab_t], lambda: pool.tensor_copy(out=zab[:, :], in_=zs[:, 13, :]))
                    dp = slice(d * 64, d * 64 + 64)
                    for half in range(2):
                        pbk = half
                        k.grp(PE, [twb_t, pc_t], [ps_t[pbk]], [mm(ps[pbk][:, c2 * ST:(c2 + 1) * ST], w2b[dp, (half * 2 + c2) * 128:(half * 2 + c2 + 1) * 128], twb[dp, :], True, True) for c2 in range(2)])
                        for c2 in range(2):
                            cc = half * 2 + c2
                            k.op(ACT, [ps_t[pbk], pk_t], [sg_t], lambda cc=cc, c2=c2, pbk=pbk: act.activation(out=sg[:, cc, :], in_=ps[pbk][:, c2 * ST:(c2 + 1) * ST], func=AF.Sigmoid, bias=W0(d * 4 + cc), scale=1.0))
                    for half in range(2):
                        pbk = 2 + half
                        k.grp(PE, [zab_t, pc_t], [ps_t[pbk]], [mm(ps[pbk][:, c2 * ST:(c2 + 1) * ST], a2b[dp, (half * 2 + c2) * 128:(half * 2 + c2 + 1) * 128], zab[dp, :], True, True) for c2 in range(2)])
                        for c2 in range(2):
                            cc = half * 2 + c2
                            k.op(ACT, [ps_t[pbk], pk_t], [aa_t], lambda cc=cc, c2=c2, pbk=pbk: act.activation(out=aa[:, cc, :], in_=ps[pbk][:, c2 * ST:(c2 + 1) * ST], func=AF.Sigmoid, bias=A0(d * 4 + cc), scale=1.0))
                    for cc in range(4):
                        for ci in range(NCK):
                            k.op(DVE, [sg_t, pc_t], [cum_t], lambda cc=cc, ci=ci: dve.tensor_tensor_scan(
                                out=cum[:, cc, ci * CH:(ci + 1) * CH], data0=ones128[:, :], data1=sg[:, cc, ci * CH:(ci + 1) * CH], initial=0.0, op0=ALU.mult, op1=ALU.add))
                    k.op(POOL, [cum_t, sg_t], [ee_t], lambda: pool.tensor_tensor(out=ee[:, :, :], in0=cum[:, :, :], in1=sg[:, :, :], op=ALU.subtract))
                    cumv = cum[:, :, :].rearrange("p c (i t) -> p c i t", t=CH)
                    totv = lambda q: tot[:, q, :].rearrange("p (c i) -> p c i", i=NCK)
                    k.op(DVE, [cum_t], [tot_t], lambda: dve.tensor_copy(out=totv(0), in_=cumv[:, :, :, CH - 1]))
                    k.op(ACT, [tot_t], [tot_t], lambda: act.activation(out=tot[:, 2, :], in_=tot[:, 0, :], func=AF.Exp, scale=-CDEC))
                    if d == 0:
                        spec = [(winc, cum, -CDEC), (winv, cum, CDEC), (wexc, ee, -CDEC)]
                    else:
                        totb = totv(0).unsqueeze(3).to_broadcast([128, 4, NCK, CH])
                        k.op(DVE, [ee_t, tot_t], [kk_t], lambda: dve.tensor_tensor(out=v4(kk), in0=v4(ee), in1=totb, op=ALU.subtract))
                        k.op(POOL, [cum_t, tot_t], [kd_t], lambda: pool.tensor_tensor(out=v4(kd), in0=v4(cum), in1=totb, op=ALU.subtract))
                        spec = [(winc, kk, CDEC), (winv, kk, -CDEC), (wexc, kd, CDEC)]
                    for (o_, i_, sc_) in spec:
                        k.op(ACT, [cum_t, ee_t, kk_t, kd_t], [wx_t], lambda o_=o_, i_=i_, sc_=sc_: act.activation(out=o_[:, :, :], in_=i_[:, :, :], func=AF.Exp, scale=sc_))
                    bc4 = lambda c0: pk2T[:, c0:c0 + 4].unsqueeze(2).to_broadcast([128, 4, ST])
                    k.op(DVE, zs_t[K_:K_ + 4] + [pk_t], [kkr_t], lambda: dve.tensor_tensor(out=kkr[:, :, :], in0=zs[:, K_:K_ + 4, :], in1=bc4(KK), op=ALU.mult))
                    k.op(POOL, [kkr_t], [sqk_t], lambda: pool.tensor_tensor(out=sqk[:, :, :], in0=kkr[:, :, :], in1=kkr[:, :, :], op=ALU.mult))
                    for half in range(2):
                        pbk = 4 + half if half == 0 else 0
                        k.grp(PE, [sqk_t, bonesb_t], [ps_t[pbk]], [mm(ps[pbk][:, c2 * ST:(c2 + 1) * ST], bonesb[:, :], sqk[:, half * 2 + c2, :], True, True) for c2 in range(2)])
                        k.op(ACT, [ps_t[pbk], cns_t], [sdk_t], lambda half=half, pbk=pbk: act.activation(out=sdk[:, half * 2:half * 2 + 2, :].rearrange("p a b -> p (a b)"), in_=ps[pbk][:, :], func=AF.Ln, bias=epsb[:, 2:3], scale=1.0))
                    k.op(ACT, [sdk_t], [sdk_t], lambda: act.activation(out=sdk[:, :, :], in_=sdk[:, :, :], func=AF.Exp, scale=-0.5))
                    k.op(POOL, [kkr_t, sdk_t], [kk_t], lambda: pool.tensor_tensor(out=kk[:, :, :], in0=kkr[:, :, :], in1=sdk[:, :, :], op=ALU.mult))
                    k.op(DVE, [aa_t, pk_t], [t1_t], lambda: dve.scalar_tensor_tensor(out=t1[:, :, :], in0=aa[:, :, :], scalar=-1.0, in1=bc4(KA), op0=ALU.add, op1=ALU.mult))
                    k.op(DVE, [t1_t] + zs_t[K_:K_ + 4], [kd_t], lambda: dve.scalar_tensor_tensor(out=kd[:, :, :], in0=t1[:, :, :], scalar=1.0, in1=zs[:, K_:K_ + 4, :], op0=ALU.add, op1=ALU.mult))
                    k.op(POOL, [kk_t, aa_t], [bb_t], lambda: pool.tensor_tensor(out=bb[:, :, :], in0=kk[:, :, :], in1=aa[:, :, :], op=ALU.mult))
                    k.op(POOL, [kd_t] + zs_t[R_:R_ + 4], [t1_t], lambda: pool.tensor_tensor(out=t1[:, :, :], in0=kd[:, :, :], in1=zs[:, R_:R_ + 4, :], op=ALU.mult))
                    k.op(DVE, [t1_t, pk_t], [rkb_t], lambda: dve.tensor_tensor(out=rkb[:, :, :], in0=t1[:, :, :], in1=bc4(RK), op=ALU.mult))
                    for half in range(2):
                        pbk = 1 + half
                        k.grp(PE, [rkb_t, bonesb_t], [ps_t[pbk]], [mm(ps[pbk][:, c2 * ST:(c2 + 1) * ST], bonesb[:, :], rkb[:, half * 2 + c2, :], True, True) for c2 in range(2)])
                        k.op(DVE, [ps_t[pbk]] + zs_t[V_:V_ + 4], [bon_t], lambda half=half, pbk=pbk: dve.tensor_tensor(
                            out=bon[:, half * 2:half * 2 + 2, :], in0=zs[:, V_ + half * 2:V_ + half * 2 + 2, :], in1=ps[pbk][:, :].rearrange("p (a b) -> p a b", b=ST), op=ALU.mult))
                    k.op(POOL, [kk_t, wx_t], [KR_t], lambda: pool.tensor_tensor(out=KR[:, :, :, 0, :], in0=v4(kk), in1=v4(wexc), op=ALU.mult))
                    k.op(DVE, zs_t[R_:R_ + 4] + [wx_t], [KR_t], lambda: dve.tensor_tensor(out=KR[:, :, :, 1, :], in0=zs[:, R_:R_ + 4, :].rearrange("p c (i t) -> p c i t", t=CH), in1=v4(winc), op=ALU.mult))
                    k.op(POOL, [kd_t, wx_t], [kt_t], lambda: pool.tensor_tensor(out=kt[:, :, :], in0=kd[:, :, :], in1=winv[:, :, :], op=ALU.mult))
                    k.op(DVE, [bb_t, wx_t], [bt_t], lambda: dve.tensor_tensor(out=bt[:, :, :], in0=bb[:, :, :], in1=winv[:, :, :], op=ALU.mult))
                    k.op(ACT, zs_t[V_:V_ + 4], [vb_t], lambda: act.activation(out=vb[:, :, :], in_=zs[:, V_:V_ + 4, :], func=AF.Copy))
                    corder = list(range(NCK) if d == 0 else range(NCK - 1, -1, -1))
                    for ci in corder:
                        cs = slice(ci * CH, (ci + 1) * CH)
                        A4, A4_t, VtZ, VtZ_t = A4c[ci], A4c_t[ci], VtZc[ci], VtZc_t[ci]
                        Ktok, Btok, KB_t = Ktokc[ci], Btokc[ci], KBc_t[ci]
                        Nb, Nb_t, NTb, NTb_t, Zb, Zb_t = Nbc[ci], Nbc_t[ci], NTbc[ci], NTbc_t[ci], Zbc[ci], Zbc_t[ci]
                        k.grp(PE, [vb_t, kt_t, identb_t], [ps_t[6]],
                              [lambda cc=cc, cs=cs: pe.transpose(psb6[:, cc * CH:(cc + 1) * CH], vb[:, cc, cs], identb[:, 0, :]) for cc in range(4)] +
                              [lambda cc=cc, cs=cs: pe.transpose(psb6[:, (4 + cc) * CH:(5 + cc) * CH], kt[:, cc, cs], identb[:, 0, :]) for cc in range(4)])
                        k.grp(PE, [bt_t, identb_t], [ps_t[5]], [lambda cc=cc, cs=cs: pe.transpose(psb5[:, cc * CH:(cc + 1) * CH], bt[:, cc, cs], identb[:, 0, :]) for cc in range(4)])
                        VtZv = VtZ[:, :, :].rearrange("p (c e) (f v) -> p c e f v", e=2, f=2)
                        p6v = psb6[:, 0:512].rearrange("p (c e v) -> p c e v", e=2, v=64)
                        for e in range(2):
                            k.op(DVE, [ps_t[6]], [VtZ_t], lambda e=e, VtZv=VtZv: dve.tensor_copy(out=VtZv[:, :, e, e, :], in_=p6v[:, :, e, :]))
                        k.op(DVE, [ps_t[6]], [KB_t], lambda Ktok=Ktok: dve.tensor_copy(out=Ktok[:, :, :].rearrange("p a b -> p (a b)"), in_=psb6[:, 512:1024]))
                        k.op(ACT, [ps_t[5]], [KB_t], lambda Btok=Btok: act.activation(out=Btok[:, :, :].rearrange("p a b -> p (a b)"), in_=psb5[:, 0:512], func=AF.Copy))
                        for h in range(8):
                            cc, e = h // 2, h % 2
                            hp = slice(e * 64, e * 64 + 64)
                            pbk = h % 2
                            krv = KR[hp, cc, ci, :, :].rearrange("p a b -> p (a b)")
                            k.grp(PE, [kt_t, bt_t, KR_t], [ps_t[pbk]], [mm(ps[pbk][:, 0:256], kt[hp, cc, cs], krv, True, True), mm(ps[pbk][:, 256:512], bt[hp, cc, cs], krv, True, True)])
                            k.op(DVE, [ps_t[pbk], pc_t], [A4_t[h]], lambda h=h, pbk=pbk, A4=A4: dve.tensor_tensor(out=A4[:, h, :], in0=ps[pbk][:, :], in1=cm[:, 0:512], op=ALU.mult))
                        NT0v = NTb[0][:, :, :].rearrange("p (c e) t -> p c e t", e=2)
                        for e in range(2):
                            pbk = 2 + e
                            hp = slice(e * 64, e * 64 + 64)
                            k.grp(PE, [bt_t, KR_t], [ps_t[pbk]], [mm(ps[pbk][:, cc * CH:(cc + 1) * CH], KR[hp, cc, ci, 0, :], bt[hp, cc, cs], True, True) for cc in range(4)])
                            k.op(DVE, [ps_t[pbk], pc_t], NTb_t[0], lambda e=e, pbk=pbk, NT0v=NT0v: dve.tensor_tensor(
                                out=NT0v[:, :, e, :], in0=ps[pbk][:, :].rearrange("p (a b) -> p a b", b=CH), in1=cmT4[:, :, :], op=ALU.mult))
                        for g in range(2):
                            k.op(POOL, A4_t[g * 4:(g + 1) * 4], [Nb_t[0][g]], lambda g=g, Nb=Nb, A4=A4: pool.tensor_copy(out=Nb[0][:, g * 4:(g + 1) * 4, :], in_=A4[:, g * 4:(g + 1) * 4, 256:384]))
                            k.op(POOL, A4_t[g * 4:(g + 1) * 4] + [identb_t], [Zb_t[0][g]], lambda g=g, Zb=Zb, A4=A4: pool.tensor_tensor(out=Zb[0][:, g * 4:(g + 1) * 4, :], in0=identb[:, :, :], in1=A4[:, g * 4:(g + 1) * 4, 256:384], op=ALU.subtract))
                    grps = [(ci, g) for ci in corder for g in range(2)]
                    cur = 0
                    for lev in range(6):
                        nxt = 1 - cur
                        last = lev == 5
                        for gi, (ci, g) in enumerate(grps):
                            bs = 3 * (gi % 2)
                            X, Y = bs, bs + 1
                            Nb, Nb_t, NTb, NTb_t = Nbc[ci], Nbc_t[ci], NTbc[ci], NTbc_t[ci]
                            hs = range(g * 4, g * 4 + 4)
                            if not last:
                                k.grp(PE, [Nb_t[cur][g], NTb_t[cur][g]], [ps_t[X]], [mm(ps[X][:, (h % 4) * CH:(h % 4 + 1) * CH], NTb[cur][:, h, :], Nb[cur][:, h, :], True, True) for h in hs])
                            k.grp(PE, [Nb_t[cur][g], NTb_t[cur][g]], [ps_t[Y]], [mm(ps[Y][:, (h % 4) * CH:(h % 4 + 1) * CH], Nb[cur][:, h, :], NTb[cur][:, h, :], True, True) for h in hs])
                            if not last:
                                k.op(ACT, [ps_t[X]], [Nb_t[nxt][g]], lambda g=g, X=X, nxt=nxt, Nb=Nb: act.activation(out=Nb[nxt][:, g * 4:(g + 1) * 4, :].rearrange("p a b -> p (a b)"), in_=ps[X][:, :], func=AF.Copy))
                            k.op(DVE, [ps_t[Y]], [NTb_t[nxt][g]], lambda g=g, Y=Y, nxt=nxt, NTb=NTb: dve.tensor_copy(out=NTb[nxt][:, g * 4:(g + 1) * 4, :].rearrange("p a b -> p (a b)"), in_=ps[Y][:, :]))
                        for gi, (ci, g) in enumerate(grps):
                            Wk = 3 * (gi % 2) + 2
                            NTb, NTb_t, Zb, Zb_t = NTbc[ci], NTbc_t[ci], Zbc[ci], Zbc_t[ci]
                            fns = []
                            for h in range(g * 4, g * 4 + 4):
                                fns.append(mm(ps[Wk][:, (h % 4) * CH:(h % 4 + 1) * CH], NTb[nxt][:, h, :], Zb[cur][:, h, :], True, False))
                                fns.append(mm(ps[Wk][:, (h % 4) * CH:(h % 4 + 1) * CH], identb[:, 0, :], Zb[cur][:, h, :], False, True))
                            k.grp(PE, [NTb_t[nxt][g], Zb_t[cur][g], identb_t], [ps_t[Wk]], fns)
                            if gi % 2 == 0:
                                k.op(ACT, [ps_t[Wk]], [Zb_t[nxt][g]], lambda g=g, Wk=Wk, nxt=nxt, Zb=Zb: act.activation(out=Zb[nxt][:, g * 4:(g + 1) * 4, :].rearrange("p a b -> p (a b)"), in_=ps[Wk][:, :], func=AF.Copy))
                            else:
                                k.op(DVE, [ps_t[Wk]], [Zb_t[nxt][g]], lambda g=g, Wk=Wk, nxt=nxt, Zb=Zb: dve.tensor_copy(out=Zb[nxt][:, g * 4:(g + 1) * 4, :].rearrange("p a b -> p (a b)"), in_=ps[Wk][:, :]))
                        cur = nxt
                    for ci in corder:
                        cs = slice(ci * CH, (ci + 1) * CH)
                        A4, A4_t, VtZ, VtZ_t = A4c[ci], A4c_t[ci], VtZc[ci], VtZc_t[ci]
                        Ktok, Btok, KB_t = Ktokc[ci], Btokc[ci], KBc_t[ci]
                        Zf, Zf_t = Zbc[ci][cur], Zbc_t[ci][cur]
                        UnZv = UnZ[:, :, :].rearrange("p (c e) (f v) -> p c e f v", e=2, f=2)
                        G0, G0_t = Gbd[gcur], Gbd_t[gcur]
                        G1, G1_t = Gbd[1 - gcur], Gbd_t[1 - gcur]
                        fns = []
                        for cc in range(4):
                            o_ = ps[0][:, cc * CH:(cc + 1) * CH]
                            fns.append(mm(o_, KR[:, cc, ci, 0, :], G0[:, cc, :], True, False))
                            for e in range(2):
                                fns.append(mm(o_, A4[:, 2 * cc + e, 0:128], VtZ[:, 2 * cc + e, :], False, e == 1))
                        k.grp(PE, [KR_t, G0_t, VtZ_t] + A4_t, [ps_t[0]], fns)
                        k.op(ACT, [ps_t[0]], [X1_t], lambda: act.activation(out=X1sb[:, :, :].rearrange("p a b -> p (a b)"), in_=ps[0][:, :], func=AF.Copy))
                        k.grp(PE, [X1_t] + Zf_t, [ps_t[1]], [mm(ps[1][:, h * 64:(h + 1) * 64], Zf[:, h, :], X1sb[:, h // 2, (h % 2) * 64:(h % 2) * 64 + 64], True, True) for h in range(8)])
                        p1v = ps[1][:, :].rearrange("p (c e v) -> p c e v", e=2, v=64)
                        for e in range(2):
                            k.op(DVE, [ps_t[1]], [UnZ_t], lambda e=e: dve.tensor_scalar(out=UnZv[:, :, e, e, :], in0=p1v[:, :, e, :], scalar1=-1.0, scalar2=None, op0=ALU.mult))
                        fns = []
                        for cc in range(4):
                            o_ = ps[3][:, cc * CH:(cc + 1) * CH]
                            fns.append(mm(o_, identb[:, 0, :], G0[:, cc, :], True, False))
                            for e in range(2):
                                h = 2 * cc + e
                                fns.append(mm(o_, Ktok[:, cc, :], VtZ[:, h, :], False, False))
                                fns.append(mm(o_, Btok[:, cc, :], UnZ[:, h, :], False, e == 1))
                        k.grp(PE, [G0_t, VtZ_t, UnZ_t, KB_t, identb_t], [ps_t[3]], fns)
                        for cc in range(4):
                            k.op(DVE, [ps_t[3], tot_t, bones_t], [G1_t], lambda cc=cc, ci=ci, G1=G1: dve.scalar_tensor_tensor(
                                out=G1[:, cc, :], in0=ps[3][:, cc * CH:(cc + 1) * CH], scalar=tot[:, 2, cc * NCK + ci:cc * NCK + ci + 1], in1=bones[:, :], op0=ALU.mult, op1=ALU.mult))
                        fns = []
                        for cc in range(4):
                            o_ = ps[2][:, cc * CH:(cc + 1) * CH]
                            fns.append(mm(o_, G0[:, cc, :], KR[:, cc, ci, 1, :], True, False))
                            for e in range(2):
                                h = 2 * cc + e
                                fns.append(mm(o_, VtZ[:, h, :], A4[:, h, 128:256], False, False))
                                fns.append(mm(o_, UnZ[:, h, :], A4[:, h, 384:512], False, e == 1))
                        k.grp(PE, [KR_t, G0_t, VtZ_t, UnZ_t] + A4_t, [ps_t[2]], fns)
                        k.op(ACT, [ps_t[2]], [ysb_t], lambda cs=cs: act.activation(out=ysb[:, :, cs], in_=ps[2][:, :].rearrange("p (a b) -> p a b", b=CH), func=AF.Copy))
                        gcur = 1 - gcur
                    yrows = lambda tns: tns.rearrange("(c p) t -> p c t", p=128)[:, :, t0:t0 + ST]
                    if not with_mix:
                        k.dma(yrows(ybT[s]), ysb[:, :, :], [ysb_t], [drt("ybT%d" % s)], ysb_t)
                        k.dma(yrows(bbT[s]), bon[:, :, :], [bon_t], [drt("bbT%d" % s)], bon_t)
                        continue
                    mix_tail(s, t0, T, zrows)
            k.barrier()

    if stage == "B0":
        scan_pass(0, False)
        return
    scan_pass(1, False)
    if stage == "B":
        return
    scan_pass(0, True)
    if stage == "C":
        return

    with ExitStack() as pd_:
        PFX[0] = "D_"
        B = alloc_ffn(pd_, False)
        yst = [sb("yst%d" % i, [128, 4, D], F32, pd_) for i in range(2)]
        yst_t = k.tls("yst", 2, dma=True)

        def load_fm(ti, xb):
            s, t0 = tiles[ti]
            for (dv, ko) in x2_views(s, t0, t0 + TT):
                k.dma(B.xfm[xb][:, ko:ko + 4, :], dv, [drt("x2T%d" % s)], B.xfm_t[xb], B.xfm_t[xb][0])

        load_fm(0, 0)
        pre_norm(B, B.xfm[0], B.xfm_t[0], tiles[0][0], 6)
        for ti, (s, t0) in enumerate(tiles):
            xb = ti % 2
            last = ti + 1 == len(tiles)
            if not last:
                load_fm(ti + 1, 1 - xb)
            xfm, xfm_t = B.xfm[xb], B.xfm_t[xb]
            ffn_ab(B, 1)
            if not last:
                pre_norm(B, B.xfm[1 - xb], B.xfm_t[1 - xb], tiles[ti + 1][0], 6)
            ffn_o(B, 1, not last)
            post_norm_residual(B, xfm, xfm_t, s, 6)
            for q in range(4):
                for half in range(2):
                    pbk = (q * 2 + half) % 4
                    k.grp(PE, xfm_t + [ident_t], [ps_t[pbk]], [lambda c=c, q=q, half=half, pbk=pbk: pe.transpose(ps[pbk][:, c * 128:(c + 1) * 128], xfm[:, half * 4 + c, q * 128:(q + 1) * 128], ident[:, :]) for c in range(4)])
                    if half == 0:
                        k.op(ACT, [ps_t[pbk]], [yst_t[xb]], lambda q=q, half=half, pbk=pbk: act.activation(out=yst[xb][:, q, half * 512:(half + 1) * 512], in_=ps[pbk][:, :], func=AF.Copy))
                    else:
                        k.op(DVE, [ps_t[pbk]], [yst_t[xb]], lambda q=q, half=half, pbk=pbk: dve.tensor_copy(out=yst[xb][:, q, half * 512:(half + 1) * 512], in_=ps[pbk][:, :]))
            k.dma(y_out[s][t0:t0 + TT, :].rearrange("(b p) d -> p b d", p=128), yst[xb][:, :, :], [yst_t[xb]], [drt("y%d" % s)], yst_t[xb])
        k.barrier()


def host_inputs(inputs, TA, TB):
    g = lambda n: np.ascontiguousarray(np.asarray(inputs[n], np.float32)[0])
    f32 = np.float32
    shared = {}
    for n in ["ada_w", "f1_w1", "f1_w3", "f1_w2", "f2_w1", "f2_w3", "f2_w2", "w_in", "w_out", "pool_w"]:
        shared[n] = g(n)
    shared["w2r"] = g("w2").reshape(128, 512)
    shared["a2r"] = g("a2").reshape(128, 512)
    g2p = np.zeros((256, 512), f32)
    g2p[:160] = g("g2")
    shared["g2p"] = g2p
    pack1 = np.zeros((128, 128), f32)
    pack1[0:72] = g("ada_b").reshape(72, 128)
    for i, n in enumerate(["n1_pre", "n1_post", "nm_pre", "nm_post", "n2_pre", "n2_post"]):
        pack1[72 + 8 * i:80 + 8 * i] = g(n).reshape(8, 128)
    shared["pack1"] = pack1
    pack2 = np.zeros((128, 128), f32)
    pack2[0:8] = g("w0").reshape(8, 128)
    pack2[8:16] = g("a0").reshape(8, 128)
    for i, n in enumerate(["k_k", "k_a", "r_k", "lnx_g", "lnx_b", "pool_scale"]):
        pack2[16 + 4 * i:20 + 4 * i] = g(n).reshape(4, 128)
    mu = np.zeros((2048,), f32)
    mu[:1952] = g("shift_mu")
    pack2[40:56] = mu.reshape(16, 128)
    shared["pack2"] = pack2
    shared["ident"] = np.eye(128, dtype=f32)
    tt = np.arange(128)
    cm = np.zeros((2, 128, 640), f32)
    for d in range(2):
        strict = (tt[:, None] < tt[None, :]) if d == 0 else (tt[:, None] > tt[None, :])
        incl = (tt[:, None] <= tt[None, :]) if d == 0 else (tt[:, None] >= tt[None, :])
        cm[d, :, 0:128] = strict
        cm[d, :, 128:256] = incl
        cm[d, :, 256:384] = strict
        cm[d, :, 384:512] = incl
        cm[d, :, 512:640] = strict.T
    shared["cmask"] = cm
    shared["bones"] = np.kron(np.eye(2, dtype=f32), np.ones((64, 64), f32))
    maps = []
    xp, xs = np.asarray(inputs["x_prompt"], f32), np.asarray(inputs["x_sample"], f32)
    cp, cs = np.asarray(inputs["c_prompt"], f32), np.asarray(inputs["c_sample"], f32)
    for i in range(xp.shape[0]):
        m = dict(shared)
        m["xa"] = np.ascontiguousarray(xp[i])
        m["xb"] = np.ascontiguousarray(xs[i])
        cpk = np.zeros((128, 128), f32)
        cpk[0:8] = cp[i].reshape(8, 128)
        cpk[8:16] = cs[i].reshape(8, 128)
        m["cpack"] = cpk
        pc = np.ones((128, 2, 4, 8), f32)
        for gi, win in enumerate((2, 4, 8, 16)):
            for j in range(8):
                t = j
                cnt = (t + win // 2) - max(t - win // 2, 0)
                pc[:, 0, gi, j] = win / cnt
                t = -8 + j
                cnt = min(t + win // 2, 0) - (t - win // 2)
                pc[:, 1, gi, j] = win / cnt
        m["pcorr"] = pc
        maps.append(m)
    return maps


_NC_CACHE = {}


def kernel(**inputs):
    xp, xs = np.asarray(inputs["x_prompt"]), np.asarray(inputs["x_sample"])
    nb, TA, TB = xp.shape[0], xp.shape[1], xs.shape[1]
    key = (TA, TB)
    if key not in _NC_CACHE:
        _NC_CACHE[key] = build(TA, TB)
    nc = _NC_CACHE[key]
    maps = host_inputs(inputs, TA, TB)
    res = run_bass_kernel_spmd(nc, maps, core_ids=list(range(nb)))
    ya = np.stack([np.asarray(r["ya"], np.float32) for r in res.results], 0)
    yb = np.stack([np.asarray(r["yb"], np.float32) for r in res.results], 0)
    return (ya, yb)
```
